# Optimizing a Trainium2 kernel written in Bass

```python
import jax, jax.numpy as jnp
from jax import lax
import numpy as np

D_MODEL = 2048
BATCH = 2
SEQ = 4096
DEPTH = 4

EXPAND = 2
D_INNER = EXPAND * D_MODEL
PLE_DIM = 256
N_MIXERS = 2
POOL_WINDOWS = (2, 4, 8, 16)
N_POOL_GROUPS = len(POOL_WINDOWS)
POOL_GROUP_DIM = D_INNER // N_POOL_GROUPS
LRU_HEADS = 16
LRU_BLOCK = D_INNER // LRU_HEADS
CONV_WIDTH = 4
LRU_C = 8.0
RMS_EPS = 1e-6
N_POOL_LAYERS = (DEPTH + 1) // 2
N_LRU_LAYERS = DEPTH // 2

kernel_name = "hybrid_pool_rglru_sandwich_ple"


def rmsnorm(x, g):
    xf = x.astype(jnp.float32)
    var = jnp.mean(xf * xf, axis=-1, keepdims=True)
    return (xf * lax.rsqrt(var + RMS_EPS) * g.astype(jnp.float32)).astype(x.dtype)


def pool_mixer(u, w_grp, b, scale):
    B, S, E = u.shape
    maxw = POOL_WINDOWS[-1]
    uf = u.astype(jnp.float32)
    cs = jnp.cumsum(uf, axis=1)
    csp = jnp.pad(cs, ((0, 0), (maxw, 0), (0, 0)))
    pos = jnp.arange(1, S + 1, dtype=jnp.int32)
    diffs = []
    for g, w in enumerate(POOL_WINDOWS):
        c0, c1 = g * POOL_GROUP_DIM, (g + 1) * POOL_GROUP_DIM
        win_sum = csp[:, maxw:, c0:c1] - csp[:, maxw - w:maxw - w + S, c0:c1]
        count = jnp.minimum(pos, w).astype(jnp.float32)[None, :, None]
        diffs.append(win_sum / count - uf[..., c0:c1])
    d = jnp.stack(diffs, axis=2).astype(u.dtype)
    y = jnp.einsum('bsgc,gcd->bsgd', d, w_grp).reshape(B, S, E) + b
    return y * scale


def causal_depthwise_conv(x, w, b):
    S = x.shape[1]
    xp = jnp.pad(x, ((0, 0), (CONV_WIDTH - 1, 0), (0, 0)))
    y = b
    for k in range(CONV_WIDTH):
        y = y + xp[:, k:k + S] * w[k]
    return y


def rglru(x, wa, ba, wx, bx, lam):
    B, S, E = x.shape
    xb = x.reshape(B, S, LRU_HEADS, LRU_BLOCK)
    r = jax.nn.sigmoid((jnp.einsum('bshi,hij->bshj', xb, wa).reshape(B, S, E) + ba).astype(jnp.float32))
    ig = jax.nn.sigmoid((jnp.einsum('bshi,hij->bshj', xb, wx).reshape(B, S, E) + bx).astype(jnp.float32))
    log_a = -LRU_C * r * jax.nn.softplus(-lam.astype(jnp.float32))
    a = jnp.exp(log_a)
    mult = jnp.sqrt(jnp.maximum(-jnp.expm1(2.0 * log_a), 0.0))
    bterm = mult * ig * x.astype(jnp.float32)

    def combine(left, right):
        a1, b1 = left
        a2, b2 = right
        return a1 * a2, a2 * b1 + b2

    _, h = lax.associative_scan(combine, (a, bterm), axis=1)
    return h.astype(x.dtype)


def setup_inputs(seed: int = 0) -> dict:
    key = jax.random.key(seed)
    ks = jax.random.split(key, 24)
    f32 = jnp.float32

    def nrm(k, shape, fan_in):
        return jax.random.normal(k, shape, f32) * (fan_in ** -0.5)

    def gain(k, shape):
        return 1.0 + 0.05 * jax.random.normal(k, shape, f32)

    def bias(k, shape):
        return 0.01 * jax.random.normal(k, shape, f32)

    a0 = jax.random.uniform(ks[15], (N_LRU_LAYERS, D_INNER), f32, 0.9, 0.999)
    lru_L = jnp.log(a0) - jnp.log1p(-a0)
    return {
        "x": jax.random.normal(ks[0], (BATCH, SEQ, D_MODEL), f32),
        "p": jax.random.normal(ks[1], (DEPTH, BATCH, SEQ, PLE_DIM), f32),
        "w_in": nrm(ks[2], (DEPTH, D_MODEL, 2 * D_INNER), D_MODEL),
        "w_out": nrm(ks[3], (DEPTH, D_INNER, D_MODEL), D_INNER),
        "g_pre": gain(ks[4], (DEPTH, D_MODEL)),
        "g_post": gain(ks[5], (DEPTH, D_MODEL)),
        "pool_w": nrm(ks[6], (N_POOL_LAYERS, N_POOL_GROUPS, POOL_GROUP_DIM, POOL_GROUP_DIM), POOL_GROUP_DIM),
        "pool_b": bias(ks[7], (N_POOL_LAYERS, D_INNER)),
        "pool_scale": gain(ks[8], (N_POOL_LAYERS, D_INNER)),
        "conv_w": nrm(ks[9], (N_LRU_LAYERS, CONV_WIDTH, D_INNER), CONV_WIDTH),
        "conv_b": bias(ks[10], (N_LRU_LAYERS, D_INNER)),
        "lru_wa": nrm(ks[11], (N_LRU_LAYERS, LRU_HEADS, LRU_BLOCK, LRU_BLOCK), LRU_BLOCK),
        "lru_ba": bias(ks[12], (N_LRU_LAYERS, D_INNER)),
        "lru_wx": nrm(ks[13], (N_LRU_LAYERS, LRU_HEADS, LRU_BLOCK, LRU_BLOCK), LRU_BLOCK),
        "lru_bx": bias(ks[14], (N_LRU_LAYERS, D_INNER)),
        "lru_L": lru_L,
        "w_ple": nrm(ks[16], (DEPTH, PLE_DIM, D_MODEL), PLE_DIM),
        "w_ple_gate": nrm(ks[17], (DEPTH, D_MODEL, D_MODEL), D_MODEL),
        "g_ple_in": gain(ks[18], (DEPTH, D_MODEL)),
        "g_ple_out": gain(ks[19], (DEPTH, D_MODEL)),
    }


def reference(x, p, w_in, w_out, g_pre, g_post, pool_w, pool_b, pool_scale,
              conv_w, conv_b, lru_wa, lru_ba, lru_wx, lru_bx, lru_L,
              w_ple, w_ple_gate, g_ple_in, g_ple_out):
    for i in range(DEPTH):
        h = rmsnorm(x, g_pre[i])
        uz = jnp.einsum('bsd,de->bse', h, w_in[i])
        u, z = uz[..., :D_INNER], uz[..., D_INNER:]
        j = i // N_MIXERS
        if i % N_MIXERS == 0:
            y = pool_mixer(u, pool_w[j], pool_b[j], pool_scale[j])
        else:
            uc = causal_depthwise_conv(u, conv_w[j], conv_b[j])
            y = rglru(uc, lru_wa[j], lru_ba[j], lru_wx[j], lru_bx[j], lru_L[j])
        y = y * jax.nn.silu(z)
        o = jnp.einsum('bse,ed->bsd', y, w_out[i])
        x = x + rmsnorm(o, g_post[i])
        gate = jax.nn.sigmoid(jnp.einsum('bsd,de->bse', rmsnorm(x, g_ple_in[i]), w_ple_gate[i]))
        e = jnp.einsum('bsk,kd->bsd', p[i], w_ple[i])
        x = x + rmsnorm(e * gate, g_ple_out[i])
    return x
```

```python
import contextlib
import numpy as np
import concourse.bass as bass
import concourse.mybir as mybir
from concourse.bass_utils import run_bass_kernel_spmd

F32 = mybir.dt.float32
BF16 = mybir.dt.bfloat16
AF = mybir.ActivationFunctionType
ALU = mybir.AluOpType

D = 2048
E = 4096
S = 4096
T = 512
NT = S // T
KD = D // 128
KE = E // 128
DEPTH = 4
PLE = 256
EPS = 1e-6
NPV = 336
HALO = 16
UW = HALO + T
NU = 18
UNIT = 1024
NTMP = 8
NUCF = 4
NUB = 3
NPS = 6
POOL_W = (2, 4, 8, 16)

ENGS = ("pe", "act", "dve", "sp", "pool")


class Buf:
    __slots__ = ("lw", "rd")

    def __init__(self):
        self.lw = None
        self.rd = {}


class DSem:
    def __init__(self, key):
        self.key = key
        self.n = 0


class Sched:
    def __init__(self):
        self.streams = {e: [] for e in ENGS}
        self.cnt = {e: 0 for e in ENGS}
        self.seen = {e: {} for e in ENGS}
        self.final = []

    def _waits(self, eng, reads, writes):
        need = {}

        def add(tok, raw):
            key, val, peng, isdma = tok
            if (not isdma) and peng == eng and not raw:
                return
            if need.get(key, 0) < val:
                need[key] = val

        for b in reads:
            if b.lw is not None:
                add(b.lw, True)
        for b in writes:
            if b.lw is not None:
                add(b.lw, False)
            for tok in b.rd.values():
                add(tok, False)
        out = []
        seen = self.seen[eng]
        for key, val in need.items():
            if seen.get(key, 0) < val:
                seen[key] = val
                out.append((key, val))
        return out

    def _commit(self, tok, reads, writes):
        for b in reads:
            old = b.rd.get(tok[0])
            if old is None or old[1] < tok[1]:
                b.rd[tok[0]] = tok
        for b in writes:
            b.lw = tok
            b.rd = {}

    def op(self, eng, fn, reads=(), writes=()):
        waits = self._waits(eng, reads, writes)
        self.cnt[eng] += 1
        key = ("E", eng)
        tok = (key, self.cnt[eng], eng, False)
        self.streams[eng].append((waits, fn, (key, 1)))
        self._commit(tok, reads, writes)
        return tok

    def dma(self, eng, dsem, fn, reads=(), writes=()):
        waits = self._waits(eng, reads, writes)
        dsem.n += 16
        tok = (dsem.key, dsem.n, eng, True)
        self.streams[eng].append((waits, fn, (dsem.key, 16)))
        self._commit(tok, reads, writes)
        return tok

    def mm(self, out_buf, out_ap, pairs, reads):
        waits = self._waits("pe", reads, [out_buf])
        self.cnt["pe"] += 1
        key = ("E", "pe")
        tok = (key, self.cnt["pe"], "pe", False)
        n = len(pairs)
        for i, (l, r) in enumerate(pairs):
            def fn(pe, l=l, r=r, st=(i == 0), sp=(i == n - 1)):
                return pe.matmul(out_ap, l, r, start=st, stop=sp)
            self.streams["pe"].append((waits if i == 0 else [], fn, (key, 1) if i == n - 1 else None))
        self._commit(tok, reads, [out_buf])
        return tok

    def mm_seq(self, out_buf, out_ap, items, common):
        n = len(items)
        key = ("E", "pe")
        allr = list(common)
        for i, (l, r, rb) in enumerate(items):
            rds = list(rb) + (list(common) if i == 0 else [])
            waits = self._waits("pe", rds, [out_buf] if i == 0 else [])
            allr += list(rb)

            def fn(pe, l=l, r=r, st=(i == 0), sp=(i == n - 1)):
                return pe.matmul(out_ap, l, r, start=st, stop=sp)
            self.streams["pe"].append((waits, fn, (key, 1) if i == n - 1 else None))
        self.cnt["pe"] += 1
        tok = (key, self.cnt["pe"], "pe", False)
        self._commit(tok, allr, [out_buf])
        return tok

    def mm1(self, out_buf, out_ap, l, r, start, stop, reads):
        writes = [out_buf] if start else []
        rds = list(reads)
        waits = self._waits("pe", rds, writes)
        self.cnt["pe"] += 1
        key = ("E", "pe")
        tok = (key, self.cnt["pe"], "pe", False)

        def fn(pe):
            return pe.matmul(out_ap, l, r, start=start, stop=stop)
        self.streams["pe"].append((waits, fn, (key, 1)))
        if start:
            self._commit(tok, rds, [out_buf])
        else:
            self._commit(tok, rds, [])
            out_buf.lw = tok
        return tok

    def emit(self, block, sems):
        def make(name):
            def body(eng):
                for waits, fn, inc in self.streams[name]:
                    for key, val in waits:
                        eng.wait_ge(sems[key], val)
                    ins = fn(eng)
                    if inc is not None:
                        ins.then_inc(sems[inc[0]], inc[1])
                if name == "sp":
                    for key, val in self.final:
                        eng.wait_ge(sems[key], val)
            return body
        block.tensor(make("pe"))
        block.scalar(make("act"))
        block.vector(make("dve"))
        block.sync(make("sp"))
        block.gpsimd(make("pool"))


class Rot:
    def __init__(self, aps):
        self.items = [(Buf(), ap) for ap in aps]
        self.i = 0

    def next(self):
        it = self.items[self.i % len(self.items)]
        self.i += 1
        return it


def build_program(n_layers=DEPTH, n_tiles=NT):
    nc = bass.Bass("TRN2", target_bir_lowering=False)
    sc = Sched()

    def din(name, shape):
        return nc.dram_tensor(name, shape, F32, kind="ExternalInput").ap()

    xT = din("xT", [D, S])
    pT = din("pT", [DEPTH * PLE, S])
    w_in = din("w_in", [DEPTH * 64 * 128, D])
    w_out = din("w_out", [DEPTH * 16 * 128, E])
    w_gate = din("w_gate", [DEPTH * 16 * 128, D])
    w_ple = din("w_ple", [DEPTH * 16 * 128, PLE])
    pool_w = din("pool_w", [2 * 4 * 8 * 128, 1024])
    lru_wa = din("lru_wa", [2 * 16 * 128, 512])
    lru_wx = din("lru_wx", [2 * 16 * 128, 512])
    pv_d = din("pv", [DEPTH * 128, NPV])
    cst_d = din("cst", [128, 64])
    yT = nc.dram_tensor("yT", [D, S], F32, kind="ExternalOutput").ap()
    xs = [nc.dram_tensor("xs0", [D, S], F32).ap(), nc.dram_tensor("xs1", [D, S], F32).ap()]
    hs = [nc.dram_tensor("hs0", [D, S], BF16).ap(), nc.dram_tensor("hs1", [D, S], BF16).ap()]

    es = contextlib.ExitStack()
    with es:
        def sb(name, shape, dt):
            return es.enter_context(nc.sbuf_tensor(name, shape, dt))

        xt = sb("xt", [128, KD, T], F32)
        hb = sb("hb", [128, KD, T], BF16)
        yb = sb("yb", [128, KE, T], BF16)
        ob = sb("ob", [128, KD, T], F32)
        obh = ob[:, :, :].bitcast(BF16)
        wsl = sb("wsl", [128, NU * UNIT], BF16)
        ubuf = sb("ubuf", [128, NUB, UW], F32)
        mixf = sb("mixf", [128, 4096 + 2 * UW], F32)
        dbuf = mixf[:, 0:4096].bitcast(BF16).rearrange("p (a b t) -> p a b t", a=2, b=8)
        wtmp = mixf[:, 4096:4096 + 2 * UW].rearrange("p (a t) -> p a t", a=2)
        ftmp = sb("ftmp", [128, NTMP, T], F32)
        hst = mixf[:, 0:4096].bitcast(BF16).rearrange("p (k t) -> p k t", k=KD)
        ucft = mixf[:, 0:NUCF * T].rearrange("p (a t) -> p a t", a=NUCF)
        sqb = sb("sqb", [128, 4, T], BF16)
        ucb = mixf[:, NUCF * T:NUCF * T + 2 * T].bitcast(BF16).rearrange("p (a t) -> p a t", a=4)
        rstd = sb("rstd", [128, T], F32)
        rstdb = sb("rstdb", [128, T], F32)
        pb = sb("pb", [128, 2, T], BF16)
        pvt = sb("pvt", [128, 2, NPV], F32)
        dvt = sb("dvt", [128, 2, 128], F32)
        carry = sb("carry", [128, KE, HALO], F32)
        state = sb("state", [128, KE], F32)
        ones = sb("ones", [128, 128], BF16)
        cst = sb("cst_sb", [128, 64], F32)
        fence = sb("fence", [128, 8], F32)
        psums = [es.enter_context(nc.psum_tensor(f"ps{i}", [128, T], F32)) for i in range(8)]

        B_x = [Buf() for _ in range(KD)]
        B_h = [Buf() for _ in range(KD)]
        B_y = [Buf() for _ in range(KE)]
        B_o = [Buf() for _ in range(KD)]
        B_w = [Buf() for _ in range(NU)]
        B_d = [[Buf() for _ in range(8)] for _ in range(2)]
        B_rstd = Buf()
        B_rstdb = Buf()
        B_hst = [Buf() for _ in range(KD)]
        B_pb = Buf()
        B_pv = [Buf(), Buf()]
        B_dv = [Buf(), Buf()]
        B_carry = [Buf() for _ in range(KE)]
        B_state = [Buf() for _ in range(KE)]
        B_ones = Buf()
        B_cst = Buf()
        B_fence = Buf()
        B_xs = [[Buf() for _ in range(n_tiles)] for _ in range(2)]
        B_hs = [[Buf() for _ in range(n_tiles)] for _ in range(2)]
        B_out = [Buf() for _ in range(n_tiles)]
        B_in = Buf()
        R_ps = Rot([psums[i][:, :] for i in range(NPS)])
        R_ss = Rot([psums[NPS + i][:, :] for i in range(2)])
        R_u = Rot([ubuf[:, i, :] for i in range(NUB)])
        R_wt = Rot([wtmp[:, i, :] for i in range(2)])
        R_t = Rot([ftmp[:, i, :] for i in range(NTMP)])
        R_ucf = Rot([ucft[:, i, :] for i in range(NUCF)])
        R_sq = Rot([sqb[:, i, :] for i in range(4)])
        R_ucb = Rot([ucb[:, i, :] for i in range(4)])

        wsem = [DSem(("D", f"w{i}")) for i in range(NU)]
        xsem = [DSem(("D", f"x{i}")) for i in range(4)]
        osem = DSem(("D", "o"))
        hssem = DSem(("D", "hs"))
        hlsem = [DSem(("D", f"hl{i}")) for i in range(4)]
        deferred = []
        psem = DSem(("D", "p"))
        vsem = DSem(("D", "v"))
        wctr = [0]

        def wload(src_ap, view):
            n = 1
            for dsz in src_ap.shape[1:]:
                n *= dsz
            nun = (n + UNIT - 1) // UNIT
            if wctr[0] + nun > NU:
                wctr[0] = 0
            s = wctr[0]
            wctr[0] += nun
            dst = wsl[:, s * UNIT:s * UNIT + n]
            if view is not None:
                dst = dst.rearrange(view[0], **view[1])
            bufs = B_w[s:s + nun]
            sc.dma("pool", wsem[s], lambda g, d=dst, a=src_ap: g.dma_start(out=d, in_=a),
                   reads=[B_in], writes=bufs)
            return bufs, dst

        def act(out, in_, func, reads, writes, bias=0.0, scale=1.0):
            sc.op("act", lambda a: a.activation(out=out, in_=in_, func=func, bias=bias, scale=scale),
                  reads=reads, writes=writes)

        def tanh_half(tb, tap, src_ap, src_bufs, hbias=None):
            if hbias is None:
                act(tap, src_ap, AF.Tanh, src_bufs, [tb], scale=0.5)
            else:
                act(tap, src_ap, AF.Tanh, src_bufs + [hbias[0]], [tb], bias=hbias[1], scale=0.5)

        class Stats:
            def __init__(self, n, lag):
                self.n, self.lag, self.i, self.pend = n, lag, 0, []
                self.ssb, self.ssap = R_ss.next()

            def add(self, src_ap, src_bufs):
                qb, qap = R_sq.next()
                act(qap, src_ap, AF.Square, src_bufs, [qb])
                self.pend.append((qb, qap))
                while len(self.pend) > self.lag:
                    self._one()

            def _one(self):
                qb, qap = self.pend.pop(0)
                sc.mm1(self.ssb, self.ssap, ones[:, :], qap, self.i == 0, self.i == self.n - 1, [qb, B_ones])
                self.i += 1

            def finish(self, mean_scale, lnbias=0.0, alt=False):
                while self.pend:
                    self._one()
                finish_rstd(self.ssb, self.ssap, mean_scale, lnbias, alt)

        def finish_rstd(ssb, ssap, mean_scale, lnbias=0.0, alt=False):
            tb, tap = R_t.next()
            act(tap, ssap, AF.Ln, [ssb], [tb], bias=EPS, scale=mean_scale)
            if alt:
                act(rstdb[:, :], tap, AF.Exp, [tb], [B_rstdb], bias=lnbias, scale=-0.5)
            else:
                act(rstd[:, :], tap, AF.Exp, [tb], [B_rstd], bias=lnbias, scale=-0.5)

        sc.op("dve", lambda v: v.memset(ones[:, :], 1.0), writes=[B_ones])
        sc.dma("sp", vsem, lambda q: q.dma_start(out=cst[:, :], in_=cst_d[:, :]), reads=[B_in], writes=[B_cst])

        pends = {}
        head_done = set()
        h_done = set()
        hs_stored = set()
        setup_done = set()

        def layer_ctx(li):
            par = li % 2
            return (li % 2 == 0, li // 2, pvt[:, par, :], dvt[:, par, :], B_pv[par], B_dv[par],
                    xT if li == 0 else xs[(li - 1) % 2], None if li == 0 else B_xs[(li - 1) % 2])

        def layer_setup(li):
            if li in setup_done:
                return
            setup_done.add(li)
            is_pool, jj, pvl, dvl, Bpv, Bdv, src, src_b = layer_ctx(li)
            sc.dma("sp", vsem, lambda q, d=pvl, a=pv_d[li * 128:(li + 1) * 128, :]: q.dma_start(out=d, in_=a),
                   reads=[B_in], writes=[Bpv])
            if is_pool:
                sc.op("dve", lambda v, d=dvl, p=pvl: v.tensor_tensor(out=d[:, 0:32], in0=p[:, 64:96], in1=p[:, 96:128], op=ALU.mult),
                      reads=[Bpv], writes=[Bdv])
            else:
                sc.op("dve", lambda v, d=dvl, p=pvl: v.tensor_scalar(out=d[:, 0:64], in0=p[:, 96:160], scalar1=0.5, scalar2=None, op0=ALU.mult),
                      reads=[Bpv], writes=[Bdv])
                act(dvl[:, 64:96], pvl[:, 160:192], AF.Exp, [Bpv], [Bdv], scale=-1.0)
                act(dvl[:, 64:96], dvl[:, 64:96], AF.Ln, [Bdv], [Bdv], bias=1.0)
                sc.op("dve", lambda v, d=dvl: v.tensor_scalar(out=d[:, 96:128], in0=d[:, 64:96], scalar1=-4.0, scalar2=None, op0=ALU.mult),
                      reads=[Bdv], writes=[Bdv])
                sc.op("dve", lambda v, d=dvl: v.tensor_scalar(out=d[:, 64:96], in0=d[:, 64:96], scalar1=-8.0, scalar2=None, op0=ALU.mult),
                      reads=[Bdv], writes=[Bdv])
            sc.op("dve", lambda v: v.memset(carry[:, :, :], 0.0), writes=B_carry)
            sc.op("dve", lambda v: v.memset(state[:, :], 0.0), writes=B_state)


        def make_load_x(li, tj):
            is_pool, jj, pvl, dvl, Bpv, Bdv, src, src_b = layer_ctx(li)
            c0 = tj * T

            def load_x(tj=tj, c0=c0):
                for q4 in range(4):
                    ks = slice(q4 * 4, q4 * 4 + 4)
                    sap = src[q4 * 512:(q4 + 1) * 512, c0:c0 + T].rearrange("(k p) t -> p k t", p=128)
                    rd = [B_in] if src_b is None else [src_b[tj]]
                    sc.dma("sp", xsem[q4], lambda q, d=xt[:, ks, :], a=sap: q.dma_start(out=d, in_=a),
                           reads=rd, writes=B_x[q4 * 4:q4 * 4 + 4])

            return load_x

        def phase_a(li, tj, part):
            is_pool, jj, pvl, dvl, Bpv, Bdv, src, src_b = layer_ctx(li)
            c0 = tj * T
            use_hload = (li >= 1)
            load_x = make_load_x(li, tj)
            if part in ('head', 'all') and use_hload:
                sc.op("dve", lambda v: v.memset(fence[:, 0:1], 0.0), writes=B_hst + [B_fence])
            if part in ('h', 'all'):
                if not use_hload:
                    load_x()
                    st = Stats(KD, 2)
                    for k in range(KD):
                        st.add(xt[:, k, :], [B_x[k]])
                    st.finish(1.0 / D)
                    for k in range(KD):
                        sc.op("dve", lambda v, k=k, p=pvl: v.scalar_tensor_tensor(
                            out=hb[:, k, :], in0=xt[:, k, :], scalar=p[:, k:k + 1], in1=rstd[:, :],
                            op0=ALU.mult, op1=ALU.mult), reads=[B_x[k], Bpv, B_rstd], writes=[B_h[k]])
                else:
                    for q4 in range(4):
                        ks = slice(q4 * 4, q4 * 4 + 4)
                        sap = hs[(li - 1) % 2][q4 * 512:(q4 + 1) * 512, c0:c0 + T].rearrange("(k p) t -> p k t", p=128)
                        sc.dma("sp", hlsem[q4], lambda q, d=hb[:, ks, :], a=sap: q.dma_start(out=d, in_=a),
                               reads=[B_hs[(li - 1) % 2][tj]], writes=B_h[q4 * 4:q4 * 4 + 4])

            wrow = li * D

            def in_chunk(c):
                a = w_in[(li * 64 + c) * 128:(li * 64 + c + 1) * 128, :]
                return wload(a, ("p (k n) -> p k n", dict(k=KD)))

            def proj_in(c):
                wb, wap = in_chunk(c)
                pbuf, pap_ = R_ps.next()
                sc.mm(pbuf, pap_, [(wap[:, k, :], hb[:, k, :]) for k in range(KD)], reads=wb + B_h)
                return pbuf, pap_

            def z_finish(m, zb, zap, tb, tap, hsb, hsap):
                sc.op("dve", lambda v, t=tap, z=zap: v.scalar_tensor_tensor(
                    out=t, in0=t, scalar=1.0, in1=z, op0=ALU.add, op1=ALU.mult), reads=[zb, tb], writes=[tb])
                sc.op("dve", lambda v, t=tap, h=hsap, m=m: v.scalar_tensor_tensor(
                    out=yb[:, m, :], in0=h, scalar=0.5, in1=t, op0=ALU.mult, op1=ALU.mult),
                    reads=[tb, hsb], writes=[B_y[m]])

            def z_to_y(m, zb, zap, hsb, hsap):
                tb, tap = R_t.next()
                tanh_half(tb, tap, zap, [zb])
                z_finish(m, zb, zap, tb, tap, hsb, hsap)

            if is_pool:
                def u_group(g):
                    w = POOL_W[g]
                    for half_pair in range(4):
                        for half in range(2):
                            m = g * 8 + half_pair * 2 + half
                            pbuf, pap_ = proj_in(m)
                            ub, uap = R_u.next()
                            act(uap[:, HALO:UW], pap_, AF.Copy, [pbuf], [ub])
                            sc.op("dve", lambda v, u=uap, m=m: v.tensor_copy(out=u[:, 0:HALO], in_=carry[:, m, :]),
                                  reads=[B_carry[m]], writes=[ub])
                            cur_b, cur = ub, uap
                            lo = -HALO
                            step = 1
                            while step < w:
                                nb, nap = R_wt.next()
                                nlo = lo + step
                                a0 = HALO + nlo
                                sc.op("dve", lambda v, o=nap, c=cur, a0=a0, s=step: v.tensor_tensor(
                                    out=o[:, a0:UW], in0=c[:, a0:UW], in1=c[:, a0 - s:UW - s], op=ALU.add),
                                    reads=[cur_b], writes=[nb])
                                cur_b, cur, lo = nb, nap, nlo
                                step *= 2
                            db = B_d[g % 2][m % 8]
                            dap = dbuf[:, g % 2, m % 8, :]
                            sc.op("dve", lambda v, o=dap, c=cur, u=uap, w=w: v.scalar_tensor_tensor(
                                out=o, in0=c[:, HALO:UW], scalar=1.0 / w, in1=u[:, HALO:UW],
                                op0=ALU.mult, op1=ALU.subtract), reads=[cur_b, ub], writes=[db])
                            if tj == 0:
                                tb, tap = R_t.next()
                                sc.op("dve", lambda v, t=tap, c=cur, g=g: v.tensor_tensor(
                                    out=t[:, 0:16], in0=c[:, HALO:HALO + 16], in1=cst[:, g * 16:(g + 1) * 16], op=ALU.mult),
                                    reads=[cur_b, B_cst], writes=[tb])
                                sc.op("dve", lambda v, o=dap, t=tap, u=uap: v.tensor_tensor(
                                    out=o[:, 0:16], in0=t[:, 0:16], in1=u[:, HALO:HALO + 16], op=ALU.subtract),
                                    reads=[tb, ub], writes=[db])
                            sc.op("dve", lambda v, u=uap, m=m: v.tensor_copy(out=carry[:, m, :], in_=u[:, T:UW]),
                                  reads=[ub], writes=[B_carry[m]])

                def pz_group(g):
                    prow = (jj * 4 + g) * 1024
                    for hf in range(1):
                        for pr in range(1):
                            for mo in range(8):
                                m = g * 8 + mo
                                a = pool_w[((jj * 4 + g) * 8 + mo) * 128:((jj * 4 + g) * 8 + mo + 1) * 128, :]
                                pwb, pwap = wload(a, ("p (k n) -> p k n", dict(k=8)))
                                zb, zap = proj_in(KE + m)
                                ybf, yap = R_ps.next()
                                sc.mm(ybf, yap, [(pwap[:, ki, :], dbuf[:, g % 2, ki, :]) for ki in range(8)],
                                      reads=pwb + B_d[g % 2])
                                hsb, hsap = R_t.next()
                                sc.op("act", lambda a_, o=hsap, i=yap, m=m, p=pvl, d=dvl: a_.activation(
                                    out=o, in_=i, func=AF.Identity, bias=d[:, m:m + 1], scale=p[:, 96 + m:97 + m]),
                                    reads=[ybf, Bpv, Bdv], writes=[hsb])
                                z_to_y(m, zb, zap, hsb, hsap)

                if part in ('head', 'all'):
                    u_group(0)
                if part in ('rest', 'all'):
                    for g in range(1, 4):
                        u_group(g)
                        pz_group(g - 1)
                    pz_group(3)
            else:
                pend = pends.setdefault((li, tj), {})

                def u_block(q):
                    res = []
                    for half in range(2):
                        m = 2 * q + half
                        pbuf, pap_ = proj_in(m)
                        ub, uap = R_u.next()
                        act(uap[:, HALO:UW], pap_, AF.Copy, [pbuf], [ub])
                        sc.op("dve", lambda v, u=uap, m=m: v.tensor_copy(out=u[:, HALO - 3:HALO], in_=carry[:, m, HALO - 3:HALO]),
                              reads=[B_carry[m]], writes=[ub])
                        ucf_b, ucf = R_ucf.next()
                        cw = 192
                        sc.op("dve", lambda v, o=ucf, u=uap, m=m, p=pvl: v.tensor_scalar(
                            out=o, in0=u[:, HALO - 3:UW - 3], scalar1=p[:, 192 + m:193 + m], scalar2=p[:, 64 + m:65 + m],
                            op0=ALU.mult, op1=ALU.add), reads=[ub, Bpv], writes=[ucf_b])
                        for kk in range(1, 4):
                            sc.op("dve", lambda v, o=ucf, u=uap, m=m, kk=kk, p=pvl: v.scalar_tensor_tensor(
                                out=o, in0=u[:, HALO - 3 + kk:UW - 3 + kk], scalar=p[:, cw + kk * 32 + m:cw + kk * 32 + m + 1],
                                in1=o, op0=ALU.mult, op1=ALU.add), reads=[ub, Bpv, ucf_b], writes=[ucf_b])
                        cb_, cap = R_ucb.next()
                        sc.op("dve", lambda v, o=cap, i=ucf: v.tensor_copy(out=o, in_=i), reads=[ucf_b], writes=[cb_])
                        sc.op("dve", lambda v, u=uap, m=m: v.tensor_copy(out=carry[:, m, HALO - 3:HALO], in_=u[:, UW - 3:UW]),
                              reads=[ub], writes=[B_carry[m]])
                        res.append((m, ucf_b, ucf, cb_, cap))
                    pend[q] = res

                gate_w = {}

                def g_block(q):
                    r0 = (jj * 16 + q) * 128
                    aa = lru_wa[r0:r0 + 128, :]
                    ax = lru_wx[r0:r0 + 128, :]
                    wab, waap = wload(aa, ("p (k n) -> p k n", dict(k=2)))
                    wxb, wxap = wload(ax, ("p (k n) -> p k n", dict(k=2)))
                    res = pend.pop(q)
                    cs = []
                    for jo in range(2):
                        m = res[jo][0]
                        rb, rap = R_ps.next()
                        sc.mm(rb, rap, [(waap[:, ii, jo * 128:(jo + 1) * 128], res[ii][4]) for ii in range(2)],
                              reads=wab + [res[0][3], res[1][3]])
                        ib, iap = R_ps.next()
                        sc.mm(ib, iap, [(wxap[:, ii, jo * 128:(jo + 1) * 128], res[ii][4]) for ii in range(2)],
                              reads=wxb + [res[0][3], res[1][3]])
                        zb, zap = proj_in(KE + m)
                        cs.append((rb, rap, ib, iap, zb, zap))
                    tm = []
                    for jo in range(2):
                        m = res[jo][0]
                        rb, rap, ib, iap, zb, zap = cs[jo]
                        t1b, t1 = R_t.next()
                        tab, ta = R_t.next()
                        t2b, t2 = R_t.next()
                        t3b, t3 = R_t.next()
                        tanh_half(t1b, t1, rap, [rb], hbias=(Bdv, dvl[:, m:m + 1]))
                        tanh_half(t2b, t2, iap, [ib], hbias=(Bdv, dvl[:, 32 + m:33 + m]))
                        tanh_half(t3b, t3, zap, [zb])
                        tm.append((t1b, t1, tab, ta, t2b, t2, t3b, t3))
                    for jo in range(2):
                        m = res[jo][0]
                        t1b, t1, tab, ta, t2b, t2, t3b, t3 = tm[jo]
                        sc.op("act", lambda a_, o=ta, i=t1, m=m, d=dvl: a_.activation(
                            out=o, in_=i, func=AF.Exp, bias=d[:, 96 + m:97 + m], scale=d[:, 96 + m:97 + m]),
                            reads=[t1b, Bdv], writes=[tab])
                        sc.op("act", lambda a_, o=t1, i=t1, m=m, d=dvl: a_.activation(
                            out=o, in_=i, func=AF.Exp, bias=d[:, 64 + m:65 + m], scale=d[:, 64 + m:65 + m]),
                            reads=[t1b, Bdv], writes=[t1b])
                        act(t1, t1, AF.Ln, [t1b], [t1b], bias=1.0, scale=-1.0)
                        act(t1, t1, AF.Exp, [t1b], [t1b], scale=0.5)
                    for jo in range(2):
                        m, ucf_b, ucf, _, _ = res[jo]
                        rb, rap, ib, iap, zb, zap = cs[jo]
                        t1b, t1, tab, ta, t2b, t2, t3b, t3 = tm[jo]
                        sc.op("dve", lambda v, a=t2, b=ucf: v.scalar_tensor_tensor(
                            out=a, in0=a, scalar=1.0, in1=b, op0=ALU.add, op1=ALU.mult),
                            reads=[t2b, ucf_b], writes=[t2b])
                        sc.op("dve", lambda v, a=t2, b=t1: v.scalar_tensor_tensor(
                            out=a, in0=a, scalar=0.5, in1=b, op0=ALU.mult, op1=ALU.mult),
                            reads=[t1b, t2b], writes=[t2b])
                        sc.op("dve", lambda v, o=t1, a=ta, b=t2, m=m: v.tensor_tensor_scan(
                            out=o, data0=a, data1=b, initial=state[:, m:m + 1], op0=ALU.mult, op1=ALU.add),
                            reads=[tab, t2b, B_state[m]], writes=[t1b])
                        sc.op("dve", lambda v, o=t1, m=m: v.tensor_copy(out=state[:, m:m + 1], in_=o[:, T - 1:T]),
                              reads=[t1b], writes=[B_state[m]])
                        z_finish(m, zb, zap, t3b, t3, t1b, t1)

                if part in ('head', 'all'):
                    u_block(0)
                    u_block(1)
                if part in ('rest', 'all'):
                    g_block(0)
                    for q in range(2, 16):
                        u_block(q)
                        g_block(q - 1)
                    g_block(15)


        for li in range(n_layers):
            is_pool = (li % 2 == 0)
            jj = li // 2
            par = li % 2
            src = xT if li == 0 else xs[(li - 1) % 2]
            src_b = None if li == 0 else B_xs[(li - 1) % 2]
            last = (li == n_layers - 1)
            dst = yT if last else xs[li % 2]
            dst_b = B_out if last else B_xs[li % 2]
            pvl = pvt[:, par, :]
            dvl = dvt[:, par, :]
            Bpv = B_pv[par]
            Bdv = B_dv[par]

            layer_setup(li)

            for tj in range(n_tiles):
                c0 = tj * T
                use_hload = (li >= 1)
                load_x = make_load_x(li, tj)
                if (li, tj) not in h_done:
                    phase_a(li, tj, 'h')
                if (li, tj) not in head_done:
                    phase_a(li, tj, 'head')
                phase_a(li, tj, 'rest')
                if tj + 1 < n_tiles:
                    nx = (li, tj + 1)
                elif li + 1 < n_layers:
                    nx = (li + 1, 0)
                else:
                    nx = None

                pap = pT[li * PLE:(li + 1) * PLE, c0:c0 + T].rearrange("(k p) t -> p k t", p=128)
                sc.dma("pool", psem, lambda g, a=pap: g.dma_start(out=pb[:, :, :], in_=a), reads=[B_in], writes=[B_pb])

                def next_h():
                    if nx is not None and nx[0] >= 1 and nx not in h_done:
                        layer_setup(nx[0])
                        phase_a(nx[0], nx[1], 'h')
                        h_done.add(nx)

                prev = deferred.pop(0) if deferred else None
                st2 = None
                if prev is not None:
                    prev[0]()
                    if prev[2]:
                        st2 = Stats(KD, 1)
                st = Stats(KD, 1 if st2 is not None else 2)
                if st2 is None or (nx is not None and (nx[0] - 1, nx[1]) in hs_stored):
                    next_h()
                orow = li * E
                x_loaded = not use_hload
                for mo in range(KD):
                    a = w_out[(li * 16 + mo) * 128:(li * 16 + mo + 1) * 128, :]
                    wb, wap = wload(a, ("p (k n) -> p k n", dict(k=KE)))
                    pbuf, pap_ = R_ps.next()
                    if mo == 0:
                        sc.mm_seq(pbuf, pap_, [(wap[:, k, :], yb[:, k, :], [B_y[k]]) for k in range(KE)], wb)
                    else:
                        sc.mm(pbuf, pap_, [(wap[:, k, :], yb[:, k, :]) for k in range(KE)], reads=wb + B_y)
                    act(ob[:, mo, :], pap_, AF.Copy, [pbuf], [B_o[mo]])
                    st.add(pap_, [pbuf])
                    if st2 is not None:
                        if 2 <= mo < 10:
                            st2.add(xt[:, 2 * (mo - 2), :], [B_x[2 * (mo - 2)]])
                            st2.add(xt[:, 2 * (mo - 2) + 1, :], [B_x[2 * (mo - 2) + 1]])
                        elif mo == 10:
                            st2.finish(1.0 / D, alt=True)
                            prev[1]()
                            if not x_loaded:
                                load_x()
                                x_loaded = True
                            prev[3]()
                            next_h()
                    if not x_loaded and (prev is None or (st2 is None and mo == 3)):
                        load_x()
                        x_loaded = True
                st.finish(1.0 / D)
                for k in range(KD):
                    sc.op("dve", lambda v, k=k, p=pvl: v.scalar_tensor_tensor(
                        out=ob[:, k, :], in0=ob[:, k, :], scalar=p[:, 16 + k:17 + k], in1=rstd[:, :],
                        op0=ALU.mult, op1=ALU.mult), reads=[B_o[k], Bpv, B_rstd], writes=[B_o[k]])
                    sc.op("dve", lambda v, k=k: v.tensor_tensor(out=xt[:, k, :], in0=xt[:, k, :], in1=ob[:, k, :], op=ALU.add),
                          reads=[B_o[k], B_x[k]], writes=[B_x[k]])

                for k in range(KD):
                    sc.op("act", lambda a_, k=k, p=pvl: a_.activation(
                        out=yb[:, k, :], in_=xt[:, k, :], func=AF.Identity, scale=p[:, 32 + k:33 + k]),
                        reads=[B_x[k], Bpv], writes=[B_y[k]])
                if nx is not None and nx[0] >= 1:
                    phase_a(nx[0], nx[1], 'head')
                    head_done.add(nx)
                st = Stats(KD, 2)
                for k in range(KD):
                    st.add(xt[:, k, :], [B_x[k]])
                st.finish(1.0 / D)
                st = Stats(KD, 2)
                grow = li * D
                for mo in range(KD):
                    a = w_gate[(li * 16 + mo) * 128:(li * 16 + mo + 1) * 128, :]
                    wb, wap = wload(a, ("p (k n) -> p k n", dict(k=KD)))
                    a2 = w_ple[(li * 16 + mo) * 128:(li * 16 + mo + 1) * 128, :]
                    pwb, pwap = wload(a2, ("p (k n) -> p k n", dict(k=2)))
                    gb, gap = R_ps.next()
                    if mo == 0:
                        sc.mm_seq(gb, gap, [(wap[:, k, :], yb[:, k, :], [B_y[k]]) for k in range(KD)], wb)
                    else:
                        sc.mm(gb, gap, [(wap[:, k, :], yb[:, k, :]) for k in range(KD)], reads=wb + B_y[0:KD])
                    eb, eap = R_ps.next()
                    sc.mm(eb, eap, [(pwap[:, k2, :], pb[:, k2, :]) for k2 in range(2)], reads=pwb + [B_pb])
                    tb, tap = R_t.next()
                    sc.op("dve", lambda v, t=tap, g_=gap: v.tensor_tensor(out=t, in0=g_, in1=rstd[:, :], op=ALU.mult),
                          reads=[gb, B_rstd], writes=[tb])
                    tanh_half(tb, tap, tap, [tb])
                    sc.op("dve", lambda v, mo=mo, e=eap, t=tap: v.scalar_tensor_tensor(
                        out=ob[:, mo, :], in0=t, scalar=1.0, in1=e, op0=ALU.add, op1=ALU.mult),
                        reads=[eb, tb], writes=[B_o[mo]])
                    st.add(ob[:, mo, :], [B_o[mo]])
                nxt = None
                if tj + 1 < n_tiles:
                    nxt = li
                elif li + 1 < n_layers:
                    nxt = li + 1

                def tail1(st=st, pvl=pvl, Bpv=Bpv, dst=dst, dst_b=dst_b, tj=tj, c0=c0, last=last):
                    st.finish(0.25 / D, float(np.log(0.5)))
                    for k in range(KD):
                        sc.op("dve", lambda v, k=k, p=pvl: v.scalar_tensor_tensor(
                            out=ob[:, k, :], in0=ob[:, k, :], scalar=p[:, 48 + k:49 + k], in1=rstd[:, :],
                            op0=ALU.mult, op1=ALU.mult), reads=[B_o[k], Bpv, B_rstd], writes=[B_o[k]])
                        sc.op("dve", lambda v, k=k: v.tensor_tensor(out=xt[:, k, :], in0=xt[:, k, :], in1=ob[:, k, :], op=ALU.add),
                              reads=[B_o[k], B_x[k]], writes=[B_x[k]])
                    for q4 in range(4):
                        ks = slice(q4 * 4, q4 * 4 + 4)
                        dap = dst[q4 * 512:(q4 + 1) * 512, c0:c0 + T].rearrange("(k p) t -> p k t", p=128)
                        tok = sc.dma("sp", osem, lambda q, s_=xt[:, ks, :], a=dap: q.dma_start(out=a, in_=s_),
                                     reads=B_x[q4 * 4:q4 * 4 + 4], writes=[dst_b[tj]])
                        if last:
                            sc.final.append((tok[0], tok[1]))

                def tail2(pvl=pvl, Bpv=Bpv, li=li, tj=tj, c0=c0):
                    alias = [b for b, _ in R_ucb.items] + [b for b, _ in R_ucf.items] + [b for b, _ in R_wt.items] \
                        + B_d[0] + B_d[1]
                    sc.op("dve", lambda v: v.memset(fence[:, 0:1], 0.0), writes=alias + [B_fence])
                    for k in range(KD):
                        sc.op("dve", lambda v, k=k, p=pvl: v.scalar_tensor_tensor(
                            out=hst[:, k, :], in0=xt[:, k, :], scalar=p[:, 320 + k:321 + k], in1=rstdb[:, :],
                            op0=ALU.mult, op1=ALU.mult), reads=[B_x[k], Bpv, B_rstdb], writes=[B_hst[k]])

                def tail2b(li=li, tj=tj, c0=c0):
                    hs_stored.add((li, tj))
                    for q4 in range(4):
                        ks = slice(q4 * 4, q4 * 4 + 4)
                        dap = hs[li % 2][q4 * 512:(q4 + 1) * 512, c0:c0 + T].rearrange("(k p) t -> p k t", p=128)
                        sc.dma("sp", hssem, lambda q, s_=hst[:, ks, :], a=dap: q.dma_start(out=a, in_=s_),
                               reads=B_hst[q4 * 4:q4 * 4 + 4], writes=[B_hs[li % 2][tj]])

                if nxt is not None and nxt >= 1:
                    deferred.append((tail1, tail2, not last, tail2b))
                else:
                    tail1()
                    if not last:
                        st2 = Stats(KD, 2)
                        for k in range(KD):
                            st2.add(xt[:, k, :], [B_x[k]])
                        st2.finish(1.0 / D, alt=True)
                        tail2()
                        tail2b()

        sems = {}
        for e in ENGS:
            sems[("E", e)] = es.enter_context(nc.semaphore(f"e_{e}"))
        for ds in wsem + xsem + hlsem + [osem, hssem, psem, vsem]:
            sems[ds.key] = es.enter_context(nc.semaphore("d_" + ds.key[1]))
        block = es.enter_context(nc.Block())
        sc.emit(block, sems)
    return nc


def _vec_pm(v):
    return np.ascontiguousarray(np.asarray(v, np.float32).reshape(-1, 128).T)


def make_inputs(b, x, p, w_in, w_out, g_pre, g_post, pool_w, pool_b, pool_scale,
                conv_w, conv_b, lru_wa, lru_ba, lru_wx, lru_bx, lru_L,
                w_ple, w_ple_gate, g_ple_in, g_ple_out):
    pv = np.zeros((DEPTH, 128, NPV), np.float32)
    for i in range(DEPTH):
        j = i // 2
        pv[i, :, 0:16] = _vec_pm(g_pre[i])
        pv[i, :, 16:32] = _vec_pm(g_post[i])
        pv[i, :, 32:48] = _vec_pm(g_ple_in[i])
        pv[i, :, 48:64] = _vec_pm(g_ple_out[i])
        if i + 1 < DEPTH:
            pv[i, :, 320:336] = _vec_pm(g_pre[i + 1])
        if i % 2 == 0:
            pv[i, :, 64:96] = _vec_pm(pool_b[j])
            pv[i, :, 96:128] = _vec_pm(pool_scale[j])
        else:
            pv[i, :, 64:96] = _vec_pm(conv_b[j])
            pv[i, :, 96:128] = _vec_pm(lru_ba[j])
            pv[i, :, 128:160] = _vec_pm(lru_bx[j])
            pv[i, :, 160:192] = _vec_pm(lru_L[j])
            for k in range(4):
                pv[i, :, 192 + k * 32:192 + (k + 1) * 32] = _vec_pm(conv_w[j, k])
    cst = np.zeros((128, 64), np.float32)
    for g, w in enumerate(POOL_W):
        cst[:, g * 16:(g + 1) * 16] = (1.0 / np.minimum(np.arange(1, 17), w)).astype(np.float32)[None, :]
    f = lambda a: np.ascontiguousarray(np.asarray(a, np.float32))

    def panels(w):
        w = np.asarray(w, np.float32)
        L, K_, M_ = w.shape
        k, m = K_ // 128, M_ // 128
        return np.ascontiguousarray(w.reshape(L, k, 128, m, 128).transpose(0, 3, 2, 1, 4)).reshape(L * m * 128, k * 128)

    def blocks(w):
        w = np.asarray(w, np.float32)
        J = w.shape[0]
        return np.ascontiguousarray(w.reshape(J, 16, 2, 128, 256).transpose(0, 1, 3, 2, 4)).reshape(J * 16 * 128, 512)
    return {
        "xT": f(np.asarray(x[b]).T),
        "pT": f(np.transpose(np.asarray(p[:, b]), (0, 2, 1)).reshape(DEPTH * PLE, S)),
        "w_in": panels(w_in),
        "w_out": panels(w_out),
        "w_gate": panels(w_ple_gate),
        "w_ple": panels(w_ple),
        "pool_w": panels(np.asarray(pool_w).reshape(8, 1024, 1024)),
        "lru_wa": blocks(lru_wa),
        "lru_wx": blocks(lru_wx),
        "pv": pv.reshape(DEPTH * 128, NPV),
        "cst": cst,
    }


N_LAUNCH_CORES = 2


def kernel(**inputs):
    nc = build_program()
    maps = [make_inputs(b, **inputs) for b in range(2)]
    in_maps = [maps[c % 2] for c in range(N_LAUNCH_CORES)]
    res = run_bass_kernel_spmd(nc, in_maps, core_ids=list(range(N_LAUNCH_CORES)))
    out = np.stack([np.ascontiguousarray(res.results[b]["yT"].T) for b in range(2)], axis=0)
    return out.astype(np.float32)
```

```python
import contextlib
import numpy as np
import concourse.bass as bass
import concourse.mybir as mybir
from concourse.bass_utils import run_bass_kernel_spmd

F32 = mybir.dt.float32
BF16 = mybir.dt.bfloat16
AF = mybir.ActivationFunctionType
ALU = mybir.AluOpType

D = 2048
E = 4096
S = 4096
T = 512
NT = S // T
KD = D // 128
KE = E // 128
DEPTH = 4
PLE = 256
EPS = 1e-6
NPV = 336
HALO = 16
UW = HALO + T
NU = 18
UNIT = 1024
NTMP = 8
NUCF = 4
NUB = 3
NPS = 6
POOL_W = (2, 4, 8, 16)

ENGS = ("pe", "act", "dve", "sp", "pool")


class Buf:
    __slots__ = ("lw", "rd")

    def __init__(self):
        self.lw = None
        self.rd = {}


class DSem:
    def __init__(self, key):
        self.key = key
        self.n = 0


class Sched:
    def __init__(self):
        self.streams = {e: [] for e in ENGS}
        self.cnt = {e: 0 for e in ENGS}
        self.seen = {e: {} for e in ENGS}
        self.final = []

    def _waits(self, eng, reads, writes):
        need = {}

        def add(tok, raw):
            key, val, peng, isdma = tok
            if (not isdma) and peng == eng and not raw:
                return
            if need.get(key, 0) < val:
                need[key] = val

        for b in reads:
            if b.lw is not None:
                add(b.lw, True)
        for b in writes:
            if b.lw is not None:
                add(b.lw, False)
            for tok in b.rd.values():
                add(tok, False)
        out = []
        seen = self.seen[eng]
        for key, val in need.items():
            if seen.get(key, 0) < val:
                seen[key] = val
                out.append((key, val))
        return out

    def _commit(self, tok, reads, writes):
        for b in reads:
            old = b.rd.get(tok[0])
            if old is None or old[1] < tok[1]:
                b.rd[tok[0]] = tok
        for b in writes:
            b.lw = tok
            b.rd = {}

    def op(self, eng, fn, reads=(), writes=()):
        waits = self._waits(eng, reads, writes)
        self.cnt[eng] += 1
        key = ("E", eng)
        tok = (key, self.cnt[eng], eng, False)
        self.streams[eng].append((waits, fn, (key, 1)))
        self._commit(tok, reads, writes)
        return tok

    def dma(self, eng, dsem, fn, reads=(), writes=()):
        waits = self._waits(eng, reads, writes)
        dsem.n += 16
        tok = (dsem.key, dsem.n, eng, True)
        self.streams[eng].append((waits, fn, (dsem.key, 16)))
        self._commit(tok, reads, writes)
        return tok

    def mm(self, out_buf, out_ap, pairs, reads):
        waits = self._waits("pe", reads, [out_buf])
        self.cnt["pe"] += 1
        key = ("E", "pe")
        tok = (key, self.cnt["pe"], "pe", False)
        n = len(pairs)
        for i, (l, r) in enumerate(pairs):
            def fn(pe, l=l, r=r, st=(i == 0), sp=(i == n - 1)):
                return pe.matmul(out_ap, l, r, start=st, stop=sp)
            self.streams["pe"].append((waits if i == 0 else [], fn, (key, 1) if i == n - 1 else None))
        self._commit(tok, reads, [out_buf])
        return tok

    def mm_seq(self, out_buf, out_ap, items, common):
        n = len(items)
        key = ("E", "pe")
        allr = list(common)
        for i, (l, r, rb) in enumerate(items):
            rds = list(rb) + (list(common) if i == 0 else [])
            waits = self._waits("pe", rds, [out_buf] if i == 0 else [])
            allr += list(rb)

            def fn(pe, l=l, r=r, st=(i == 0), sp=(i == n - 1)):
                return pe.matmul(out_ap, l, r, start=st, stop=sp)
            self.streams["pe"].append((waits, fn, (key, 1) if i == n - 1 else None))
        self.cnt["pe"] += 1
        tok = (key, self.cnt["pe"], "pe", False)
        self._commit(tok, allr, [out_buf])
        return tok

    def mm1(self, out_buf, out_ap, l, r, start, stop, reads):
        writes = [out_buf] if start else []
        rds = list(reads)
        waits = self._waits("pe", rds, writes)
        self.cnt["pe"] += 1
        key = ("E", "pe")
        tok = (key, self.cnt["pe"], "pe", False)

        def fn(pe):
            return pe.matmul(out_ap, l, r, start=start, stop=stop)
        self.streams["pe"].append((waits, fn, (key, 1)))
        if start:
            self._commit(tok, rds, [out_buf])
        else:
            self._commit(tok, rds, [])
            out_buf.lw = tok
        return tok

    def emit(self, block, sems):
        def make(name):
            def body(eng):
                for waits, fn, inc in self.streams[name]:
                    for key, val in waits:
                        eng.wait_ge(sems[key], val)
                    ins = fn(eng)
                    if inc is not None:
                        ins.then_inc(sems[inc[0]], inc[1])
                if name == "sp":
                    for key, val in self.final:
                        eng.wait_ge(sems[key], val)
            return body
        block.tensor(make("pe"))
        block.scalar(make("act"))
        block.vector(make("dve"))
        block.sync(make("sp"))
        block.gpsimd(make("pool"))


class Rot:
    def __init__(self, aps):
        self.items = [(Buf(), ap) for ap in aps]
        self.i = 0

    def next(self):
        it = self.items[self.i % len(self.items)]
        self.i += 1
        return it


def build_program(n_layers=DEPTH, n_tiles=NT):
    nc = bass.Bass("TRN2", target_bir_lowering=False)
    sc = Sched()

    def din(name, shape):
        return nc.dram_tensor(name, shape, F32, kind="ExternalInput").ap()

    xT = din("xT", [D, S])
    pT = din("pT", [DEPTH * PLE, S])
    w_in = din("w_in", [DEPTH * 64 * 128, D])
    w_out = din("w_out", [DEPTH * 16 * 128, E])
    w_gate = din("w_gate", [DEPTH * 16 * 128, D])
    w_ple = din("w_ple", [DEPTH * 16 * 128, PLE])
    pool_w = din("pool_w", [2 * 4 * 8 * 128, 1024])
    lru_wa = din("lru_wa", [2 * 16 * 128, 512])
    lru_wx = din("lru_wx", [2 * 16 * 128, 512])
    pv_d = din("pv", [DEPTH * 128, NPV])
    cst_d = din("cst", [128, 64])
    yT = nc.dram_tensor("yT", [D, S], F32, kind="ExternalOutput").ap()
    xs = [nc.dram_tensor("xs0", [D, S], F32).ap(), nc.dram_tensor("xs1", [D, S], F32).ap()]
    hs = [nc.dram_tensor("hs0", [D, S], BF16).ap(), nc.dram_tensor("hs1", [D, S], BF16).ap()]

    es = contextlib.ExitStack()
    with es:
        def sb(name, shape, dt):
            return es.enter_context(nc.sbuf_tensor(name, shape, dt))

        xt = sb("xt", [128, KD, T], F32)
        hb = sb("hb", [128, KD, T], BF16)
        yb = sb("yb", [128, KE, T], BF16)
        ob = sb("ob", [128, KD, T], F32)
        obh = ob[:, :, :].bitcast(BF16)
        wsl = sb("wsl", [128, NU * UNIT], BF16)
        ubuf = sb("ubuf", [128, NUB, UW], F32)
        mixf = sb("mixf", [128, 4096 + 2 * UW], F32)
        dbuf = mixf[:, 0:4096].bitcast(BF16).rearrange("p (a b t) -> p a b t", a=2, b=8)
        wtmp = mixf[:, 4096:4096 + 2 * UW].rearrange("p (a t) -> p a t", a=2)
        ftmp = sb("ftmp", [128, NTMP, T], F32)
        hst = mixf[:, 0:4096].bitcast(BF16).rearrange("p (k t) -> p k t", k=KD)
        ucft = mixf[:, 0:NUCF * T].rearrange("p (a t) -> p a t", a=NUCF)
        sqb = sb("sqb", [128, 4, T], BF16)
        ucb = mixf[:, NUCF * T:NUCF * T + 2 * T].bitcast(BF16).rearrange("p (a t) -> p a t", a=4)
        rstd = sb("rstd", [128, T], F32)
        rstdb = sb("rstdb", [128, T], F32)
        pb = sb("pb", [128, 2, T], BF16)
        pvt = sb("pvt", [128, 2, NPV], F32)
        dvt = sb("dvt", [128, 2, 128], F32)
        carry = sb("carry", [128, KE, HALO], F32)
        state = sb("state", [128, KE], F32)
        ones = sb("ones", [128, 128], BF16)
        cst = sb("cst_sb", [128, 64], F32)
        fence = sb("fence", [128, 8], F32)
        psums = [es.enter_context(nc.psum_tensor(f"ps{i}", [128, T], F32)) for i in range(8)]

        B_x = [Buf() for _ in range(KD)]
        B_h = [Buf() for _ in range(KD)]
        B_y = [Buf() for _ in range(KE)]
        B_o = [Buf() for _ in range(KD)]
        B_w = [Buf() for _ in range(NU)]
        B_d = [[Buf() for _ in range(8)] for _ in range(2)]
        B_rstd = Buf()
        B_rstdb = Buf()
        B_hst = [Buf() for _ in range(KD)]
        B_pb = Buf()
        B_pv = [Buf(), Buf()]
        B_dv = [Buf(), Buf()]
        B_carry = [Buf() for _ in range(KE)]
        B_state = [Buf() for _ in range(KE)]
        B_ones = Buf()
        B_cst = Buf()
        B_fence = Buf()
        B_xs = [[Buf() for _ in range(n_tiles)] for _ in range(2)]
        B_hs = [[Buf() for _ in range(n_tiles)] for _ in range(2)]
        B_out = [Buf() for _ in range(n_tiles)]
        B_in = Buf()
        R_ps = Rot([psums[i][:, :] for i in range(NPS)])
        R_ss = Rot([psums[NPS + i][:, :] for i in range(2)])
        R_u = Rot([ubuf[:, i, :] for i in range(NUB)])
        R_wt = Rot([wtmp[:, i, :] for i in range(2)])
        R_t = Rot([ftmp[:, i, :] for i in range(NTMP)])
        R_ucf = Rot([ucft[:, i, :] for i in range(NUCF)])
        R_sq = Rot([sqb[:, i, :] for i in range(4)])
        R_ucb = Rot([ucb[:, i, :] for i in range(4)])

        wsem = [DSem(("D", f"w{i}")) for i in range(NU)]
        xsem = [DSem(("D", f"x{i}")) for i in range(4)]
        osem = DSem(("D", "o"))
        hssem = DSem(("D", "hs"))
        hlsem = [DSem(("D", f"hl{i}")) for i in range(4)]
        deferred = []
        psem = DSem(("D", "p"))
        vsem = DSem(("D", "v"))
        wctr = [0]

        def wload(src_ap, view):
            n = 1
            for dsz in src_ap.shape[1:]:
                n *= dsz
            nun = (n + UNIT - 1) // UNIT
            if wctr[0] + nun > NU:
                wctr[0] = 0
            s = wctr[0]
            wctr[0] += nun
            dst = wsl[:, s * UNIT:s * UNIT + n]
            if view is not None:
                dst = dst.rearrange(view[0], **view[1])
            bufs = B_w[s:s + nun]
            sc.dma("pool", wsem[s], lambda g, d=dst, a=src_ap: g.dma_start(out=d, in_=a),
                   reads=[B_in], writes=bufs)
            return bufs, dst

        def act(out, in_, func, reads, writes, bias=0.0, scale=1.0):
            sc.op("act", lambda a: a.activation(out=out, in_=in_, func=func, bias=bias, scale=scale),
                  reads=reads, writes=writes)

        def tanh_half(tb, tap, src_ap, src_bufs, hbias=None):
            if hbias is None:
                act(tap, src_ap, AF.Tanh, src_bufs, [tb], scale=0.5)
            else:
                act(tap, src_ap, AF.Tanh, src_bufs + [hbias[0]], [tb], bias=hbias[1], scale=0.5)

        class Stats:
            def __init__(self, n, lag):
                self.n, self.lag, self.i, self.pend = n, lag, 0, []
                self.ssb, self.ssap = R_ss.next()

            def add(self, src_ap, src_bufs):
                qb, qap = R_sq.next()
                act(qap, src_ap, AF.Square, src_bufs, [qb])
                self.pend.append((qb, qap))
                while len(self.pend) > self.lag:
                    self._one()

            def _one(self):
                qb, qap = self.pend.pop(0)
                sc.mm1(self.ssb, self.ssap, ones[:, :], qap, self.i == 0, self.i == self.n - 1, [qb, B_ones])
                self.i += 1

            def finish(self, mean_scale, lnbias=0.0, alt=False):
                while self.pend:
                    self._one()
                finish_rstd(self.ssb, self.ssap, mean_scale, lnbias, alt)

        def finish_rstd(ssb, ssap, mean_scale, lnbias=0.0, alt=False):
            tb, tap = R_t.next()
            act(tap, ssap, AF.Ln, [ssb], [tb], bias=EPS, scale=mean_scale)
            if alt:
                act(rstdb[:, :], tap, AF.Exp, [tb], [B_rstdb], bias=lnbias, scale=-0.5)
            else:
                act(rstd[:, :], tap, AF.Exp, [tb], [B_rstd], bias=lnbias, scale=-0.5)

        sc.op("dve", lambda v: v.memset(ones[:, :], 1.0), writes=[B_ones])
        sc.dma("sp", vsem, lambda q: q.dma_start(out=cst[:, :], in_=cst_d[:, :]), reads=[B_in], writes=[B_cst])

        pends = {}
        head_done = set()
        h_done = set()
        hs_stored = set()
        setup_done = set()

        def layer_ctx(li):
            par = li % 2
            return (li % 2 == 0, li // 2, pvt[:, par, :], dvt[:, par, :], B_pv[par], B_dv[par],
                    xT if li == 0 else xs[(li - 1) % 2], None if li == 0 else B_xs[(li - 1) % 2])

        def layer_setup(li):
            if li in setup_done:
                return
            setup_done.add(li)
            is_pool, jj, pvl, dvl, Bpv, Bdv, src, src_b = layer_ctx(li)
            sc.dma("sp", vsem, lambda q, d=pvl, a=pv_d[li * 128:(li + 1) * 128, :]: q.dma_start(out=d, in_=a),
                   reads=[B_in], writes=[Bpv])
            if is_pool:
                sc.op("dve", lambda v, d=dvl, p=pvl: v.tensor_tensor(out=d[:, 0:32], in0=p[:, 64:96], in1=p[:, 96:128], op=ALU.mult),
                      reads=[Bpv], writes=[Bdv])
            else:
                sc.op("dve", lambda v, d=dvl, p=pvl: v.tensor_scalar(out=d[:, 0:64], in0=p[:, 96:160], scalar1=0.5, scalar2=None, op0=ALU.mult),
                      reads=[Bpv], writes=[Bdv])
                act(dvl[:, 64:96], pvl[:, 160:192], AF.Exp, [Bpv], [Bdv], scale=-1.0)
                act(dvl[:, 64:96], dvl[:, 64:96], AF.Ln, [Bdv], [Bdv], bias=1.0)
                sc.op("dve", lambda v, d=dvl: v.tensor_scalar(out=d[:, 96:128], in0=d[:, 64:96], scalar1=-4.0, scalar2=None, op0=ALU.mult),
                      reads=[Bdv], writes=[Bdv])
                sc.op("dve", lambda v, d=dvl: v.tensor_scalar(out=d[:, 64:96], in0=d[:, 64:96], scalar1=-8.0, scalar2=None, op0=ALU.mult),
                      reads=[Bdv], writes=[Bdv])
            sc.op("dve", lambda v: v.memset(carry[:, :, :], 0.0), writes=B_carry)
            sc.op("dve", lambda v: v.memset(state[:, :], 0.0), writes=B_state)


        def make_load_x(li, tj):
            is_pool, jj, pvl, dvl, Bpv, Bdv, src, src_b = layer_ctx(li)
            c0 = tj * T

            def load_x(tj=tj, c0=c0):
                for q4 in range(4):
                    ks = slice(q4 * 4, q4 * 4 + 4)
                    sap = src[q4 * 512:(q4 + 1) * 512, c0:c0 + T].rearrange("(k p) t -> p k t", p=128)
                    rd = [B_in] if src_b is None else [src_b[tj]]
                    sc.dma("sp", xsem[q4], lambda q, d=xt[:, ks, :], a=sap: q.dma_start(out=d, in_=a),
                           reads=rd, writes=B_x[q4 * 4:q4 * 4 + 4])

            return load_x

        def phase_a(li, tj, part):
            is_pool, jj, pvl, dvl, Bpv, Bdv, src, src_b = layer_ctx(li)
            c0 = tj * T
            use_hload = (li >= 1)
            load_x = make_load_x(li, tj)
            if part in ('head', 'all'):
                sc.op("dve", lambda v: v.memset(fence[:, 0:1], 0.0), writes=B_hst + [B_fence])
            if part in ('h', 'all'):
                if not use_hload:
                    load_x()
                    st = Stats(KD, 2)
                    for k in range(KD):
                        st.add(xt[:, k, :], [B_x[k]])
                    st.finish(1.0 / D)
                    for k in range(KD):
                        sc.op("dve", lambda v, k=k, p=pvl: v.scalar_tensor_tensor(
                            out=hb[:, k, :], in0=xt[:, k, :], scalar=p[:, k:k + 1], in1=rstd[:, :],
                            op0=ALU.mult, op1=ALU.mult), reads=[B_x[k], Bpv, B_rstd], writes=[B_h[k]])
                else:
                    for q4 in range(4):
                        ks = slice(q4 * 4, q4 * 4 + 4)
                        sap = hs[(li - 1) % 2][q4 * 512:(q4 + 1) * 512, c0:c0 + T].rearrange("(k p) t -> p k t", p=128)
                        sc.dma("sp", hlsem[q4], lambda q, d=hb[:, ks, :], a=sap: q.dma_start(out=d, in_=a),
                               reads=[B_hs[(li - 1) % 2][tj]], writes=B_h[q4 * 4:q4 * 4 + 4])

            wrow = li * D

            def in_chunk(c):
                a = w_in[(li * 64 + c) * 128:(li * 64 + c + 1) * 128, :]
                return wload(a, ("p (k n) -> p k n", dict(k=KD)))

            def proj_in(c):
                wb, wap = in_chunk(c)
                pbuf, pap_ = R_ps.next()
                sc.mm(pbuf, pap_, [(wap[:, k, :], hb[:, k, :]) for k in range(KD)], reads=wb + B_h)
                return pbuf, pap_

            def z_finish(m, zb, zap, tb, tap, hsb, hsap):
                sc.op("dve", lambda v, t=tap, z=zap: v.scalar_tensor_tensor(
                    out=t, in0=t, scalar=1.0, in1=z, op0=ALU.add, op1=ALU.mult), reads=[zb, tb], writes=[tb])
                sc.op("dve", lambda v, t=tap, h=hsap, m=m: v.scalar_tensor_tensor(
                    out=yb[:, m, :], in0=h, scalar=0.5, in1=t, op0=ALU.mult, op1=ALU.mult),
                    reads=[tb, hsb], writes=[B_y[m]])

            def z_to_y(m, zb, zap, hsb, hsap):
                tb, tap = R_t.next()
                tanh_half(tb, tap, zap, [zb])
                z_finish(m, zb, zap, tb, tap, hsb, hsap)

            if is_pool:
                def u_group(g):
                    w = POOL_W[g]
                    for half_pair in range(4):
                        for half in range(2):
                            m = g * 8 + half_pair * 2 + half
                            pbuf, pap_ = proj_in(m)
                            ub, uap = R_u.next()
                            act(uap[:, HALO:UW], pap_, AF.Copy, [pbuf], [ub])
                            sc.op("dve", lambda v, u=uap, m=m: v.tensor_copy(out=u[:, 0:HALO], in_=carry[:, m, :]),
                                  reads=[B_carry[m]], writes=[ub])
                            cur_b, cur = ub, uap
                            lo = -HALO
                            step = 1
                            while step < w:
                                nb, nap = R_wt.next()
                                nlo = lo + step
                                a0 = HALO + nlo
                                sc.op("dve", lambda v, o=nap, c=cur, a0=a0, s=step: v.tensor_tensor(
                                    out=o[:, a0:UW], in0=c[:, a0:UW], in1=c[:, a0 - s:UW - s], op=ALU.add),
                                    reads=[cur_b], writes=[nb])
                                cur_b, cur, lo = nb, nap, nlo
                                step *= 2
                            db = B_d[g % 2][m % 8]
                            dap = dbuf[:, g % 2, m % 8, :]
                            sc.op("dve", lambda v, o=dap, c=cur, u=uap, w=w: v.scalar_tensor_tensor(
                                out=o, in0=c[:, HALO:UW], scalar=1.0 / w, in1=u[:, HALO:UW],
                                op0=ALU.mult, op1=ALU.subtract), reads=[cur_b, ub], writes=[db])
                            if tj == 0:
                                tb, tap = R_t.next()
                                sc.op("dve", lambda v, t=tap, c=cur, g=g: v.tensor_tensor(
                                    out=t[:, 0:16], in0=c[:, HALO:HALO + 16], in1=cst[:, g * 16:(g + 1) * 16], op=ALU.mult),
                                    reads=[cur_b, B_cst], writes=[tb])
                                sc.op("dve", lambda v, o=dap, t=tap, u=uap: v.tensor_tensor(
                                    out=o[:, 0:16], in0=t[:, 0:16], in1=u[:, HALO:HALO + 16], op=ALU.subtract),
                                    reads=[tb, ub], writes=[db])
                            sc.op("dve", lambda v, u=uap, m=m: v.tensor_copy(out=carry[:, m, :], in_=u[:, T:UW]),
                                  reads=[ub], writes=[B_carry[m]])

                def pz_group(g):
                    prow = (jj * 4 + g) * 1024
                    for hf in range(1):
                        for pr in range(1):
                            for mo in range(8):
                                m = g * 8 + mo
                                a = pool_w[((jj * 4 + g) * 8 + mo) * 128:((jj * 4 + g) * 8 + mo + 1) * 128, :]
                                pwb, pwap = wload(a, ("p (k n) -> p k n", dict(k=8)))
                                zb, zap = proj_in(KE + m)
                                ybf, yap = R_ps.next()
                                sc.mm(ybf, yap, [(pwap[:, ki, :], dbuf[:, g % 2, ki, :]) for ki in range(8)],
                                      reads=pwb + B_d[g % 2])
                                hsb, hsap = R_t.next()
                                sc.op("act", lambda a_, o=hsap, i=yap, m=m, p=pvl, d=dvl: a_.activation(
                                    out=o, in_=i, func=AF.Identity, bias=d[:, m:m + 1], scale=p[:, 96 + m:97 + m]),
                                    reads=[ybf, Bpv, Bdv], writes=[hsb])
                                z_to_y(m, zb, zap, hsb, hsap)

                if part in ('head', 'all'):
                    u_group(0)
                if part in ('rest', 'all'):
                    for g in range(1, 4):
                        u_group(g)
                        pz_group(g - 1)
                    pz_group(3)
            else:
                pend = pends.setdefault((li, tj), {})

                def u_block(q):
                    res = []
                    for half in range(2):
                        m = 2 * q + half
                        pbuf, pap_ = proj_in(m)
                        ub, uap = R_u.next()
                        act(uap[:, HALO:UW], pap_, AF.Copy, [pbuf], [ub])
                        sc.op("dve", lambda v, u=uap, m=m: v.tensor_copy(out=u[:, HALO - 3:HALO], in_=carry[:, m, HALO - 3:HALO]),
                              reads=[B_carry[m]], writes=[ub])
                        ucf_b, ucf = R_ucf.next()
                        cw = 192
                        sc.op("dve", lambda v, o=ucf, u=uap, m=m, p=pvl: v.tensor_scalar(
                            out=o, in0=u[:, HALO - 3:UW - 3], scalar1=p[:, 192 + m:193 + m], scalar2=p[:, 64 + m:65 + m],
                            op0=ALU.mult, op1=ALU.add), reads=[ub, Bpv], writes=[ucf_b])
                        for kk in range(1, 4):
                            sc.op("dve", lambda v, o=ucf, u=uap, m=m, kk=kk, p=pvl: v.scalar_tensor_tensor(
                                out=o, in0=u[:, HALO - 3 + kk:UW - 3 + kk], scalar=p[:, cw + kk * 32 + m:cw + kk * 32 + m + 1],
                                in1=o, op0=ALU.mult, op1=ALU.add), reads=[ub, Bpv, ucf_b], writes=[ucf_b])
                        cb_, cap = R_ucb.next()
                        sc.op("dve", lambda v, o=cap, i=ucf: v.tensor_copy(out=o, in_=i), reads=[ucf_b], writes=[cb_])
                        sc.op("dve", lambda v, u=uap, m=m: v.tensor_copy(out=carry[:, m, HALO - 3:HALO], in_=u[:, UW - 3:UW]),
                              reads=[ub], writes=[B_carry[m]])
                        res.append((m, ucf_b, ucf, cb_, cap))
                    pend[q] = res

                gate_w = {}

                def g_block(q):
                    r0 = (jj * 16 + q) * 128
                    aa = lru_wa[r0:r0 + 128, :]
                    ax = lru_wx[r0:r0 + 128, :]
                    wab, waap = wload(aa, ("p (k n) -> p k n", dict(k=2)))
                    wxb, wxap = wload(ax, ("p (k n) -> p k n", dict(k=2)))
                    res = pend.pop(q)
                    cs = []
                    for jo in range(2):
                        m = res[jo][0]
                        rb, rap = R_ps.next()
                        sc.mm(rb, rap, [(waap[:, ii, jo * 128:(jo + 1) * 128], res[ii][4]) for ii in range(2)],
                              reads=wab + [res[0][3], res[1][3]])
                        ib, iap = R_ps.next()
                        sc.mm(ib, iap, [(wxap[:, ii, jo * 128:(jo + 1) * 128], res[ii][4]) for ii in range(2)],
                              reads=wxb + [res[0][3], res[1][3]])
                        zb, zap = proj_in(KE + m)
                        cs.append((rb, rap, ib, iap, zb, zap))
                    tm = []
                    for jo in range(2):
                        m = res[jo][0]
                        rb, rap, ib, iap, zb, zap = cs[jo]
                        t1b, t1 = R_t.next()
                        tab, ta = R_t.next()
                        t2b, t2 = R_t.next()
                        t3b, t3 = R_t.next()
                        tanh_half(t1b, t1, rap, [rb], hbias=(Bdv, dvl[:, m:m + 1]))
                        tanh_half(t2b, t2, iap, [ib], hbias=(Bdv, dvl[:, 32 + m:33 + m]))
                        tanh_half(t3b, t3, zap, [zb])
                        tm.append((t1b, t1, tab, ta, t2b, t2, t3b, t3))
                    for jo in range(2):
                        m = res[jo][0]
                        t1b, t1, tab, ta, t2b, t2, t3b, t3 = tm[jo]
                        sc.op("act", lambda a_, o=ta, i=t1, m=m, d=dvl: a_.activation(
                            out=o, in_=i, func=AF.Exp, bias=d[:, 96 + m:97 + m], scale=d[:, 96 + m:97 + m]),
                            reads=[t1b, Bdv], writes=[tab])
                        sc.op("act", lambda a_, o=t1, i=t1, m=m, d=dvl: a_.activation(
                            out=o, in_=i, func=AF.Exp, bias=d[:, 64 + m:65 + m], scale=d[:, 64 + m:65 + m]),
                            reads=[t1b, Bdv], writes=[t1b])
                        act(t1, t1, AF.Ln, [t1b], [t1b], bias=1.0, scale=-1.0)
                        act(t1, t1, AF.Exp, [t1b], [t1b], scale=0.5)
                    for jo in range(2):
                        m, ucf_b, ucf, _, _ = res[jo]
                        rb, rap, ib, iap, zb, zap = cs[jo]
                        t1b, t1, tab, ta, t2b, t2, t3b, t3 = tm[jo]
                        sc.op("dve", lambda v, a=t2, b=ucf: v.scalar_tensor_tensor(
                            out=a, in0=a, scalar=1.0, in1=b, op0=ALU.add, op1=ALU.mult),
                            reads=[t2b, ucf_b], writes=[t2b])
                        sc.op("dve", lambda v, a=t2, b=t1: v.scalar_tensor_tensor(
                            out=a, in0=a, scalar=0.5, in1=b, op0=ALU.mult, op1=ALU.mult),
                            reads=[t1b, t2b], writes=[t2b])
                        sc.op("dve", lambda v, o=t1, a=ta, b=t2, m=m: v.tensor_tensor_scan(
                            out=o, data0=a, data1=b, initial=state[:, m:m + 1], op0=ALU.mult, op1=ALU.add),
                            reads=[tab, t2b, B_state[m]], writes=[t1b])
                        sc.op("dve", lambda v, o=t1, m=m: v.tensor_copy(out=state[:, m:m + 1], in_=o[:, T - 1:T]),
                              reads=[t1b], writes=[B_state[m]])
                        z_finish(m, zb, zap, t3b, t3, t1b, t1)

                if part in ('head', 'all'):
                    u_block(0)
                    u_block(1)
                if part in ('rest', 'all'):
                    g_block(0)
                    for q in range(2, 16):
                        u_block(q)
                        g_block(q - 1)
                    g_block(15)


        for li in range(n_layers):
            is_pool = (li % 2 == 0)
            jj = li // 2
            par = li % 2
            src = xT if li == 0 else xs[(li - 1) % 2]
            src_b = None if li == 0 else B_xs[(li - 1) % 2]
            last = (li == n_layers - 1)
            dst = yT if last else xs[li % 2]
            dst_b = B_out if last else B_xs[li % 2]
            pvl = pvt[:, par, :]
            dvl = dvt[:, par, :]
            Bpv = B_pv[par]
            Bdv = B_dv[par]

            layer_setup(li)

            for tj in range(n_tiles):
                c0 = tj * T
                use_hload = (li >= 1)
                load_x = make_load_x(li, tj)
                if (li, tj) not in h_done:
                    phase_a(li, tj, 'h')
                if (li, tj) not in head_done:
                    phase_a(li, tj, 'head')
                phase_a(li, tj, 'rest')
                if tj + 1 < n_tiles:
                    nx = (li, tj + 1)
                elif li + 1 < n_layers:
                    nx = (li + 1, 0)
                else:
                    nx = None

                pap = pT[li * PLE:(li + 1) * PLE, c0:c0 + T].rearrange("(k p) t -> p k t", p=128)
                sc.dma("pool", psem, lambda g, a=pap: g.dma_start(out=pb[:, :, :], in_=a), reads=[B_in], writes=[B_pb])

                def next_h():
                    if nx is not None and nx[0] >= 1 and nx not in h_done:
                        layer_setup(nx[0])
                        phase_a(nx[0], nx[1], 'h')
                        h_done.add(nx)

                prev = deferred.pop(0) if deferred else None
                st2 = None
                if prev is not None:
                    prev[0]()
                    if prev[2]:
                        st2 = Stats(KD, 1)
                st = Stats(KD, 1 if st2 is not None else 2)
                if st2 is None or (nx is not None and (nx[0] - 1, nx[1]) in hs_stored):
                    next_h()
                orow = li * E
                x_loaded = not use_hload
                for mo in range(KD):
                    a = w_out[(li * 16 + mo) * 128:(li * 16 + mo + 1) * 128, :]
                    wb, wap = wload(a, ("p (k n) -> p k n", dict(k=KE)))
                    pbuf, pap_ = R_ps.next()
                    if mo == 0:
                        sc.mm_seq(pbuf, pap_, [(wap[:, k, :], yb[:, k, :], [B_y[k]]) for k in range(KE)], wb)
                    else:
                        sc.mm(pbuf, pap_, [(wap[:, k, :], yb[:, k, :]) for k in range(KE)], reads=wb + B_y)
                    act(ob[:, mo, :], pap_, AF.Copy, [pbuf], [B_o[mo]])
                    st.add(pap_, [pbuf])
                    if st2 is not None:
                        if 2 <= mo < 10:
                            st2.add(xt[:, 2 * (mo - 2), :], [B_x[2 * (mo - 2)]])
                            st2.add(xt[:, 2 * (mo - 2) + 1, :], [B_x[2 * (mo - 2) + 1]])
                        elif mo == 10:
                            st2.finish(1.0 / D, alt=True)
                            prev[1]()
                            if not x_loaded:
                                load_x()
                                x_loaded = True
                            prev[3]()
                            next_h()
                    if not x_loaded and (prev is None or (st2 is None and mo == 3)):
                        load_x()
                        x_loaded = True
                st.finish(1.0 / D)
                for k in range(KD):
                    sc.op("dve", lambda v, k=k, p=pvl: v.scalar_tensor_tensor(
                        out=ob[:, k, :], in0=ob[:, k, :], scalar=p[:, 16 + k:17 + k], in1=rstd[:, :],
                        op0=ALU.mult, op1=ALU.mult), reads=[B_o[k], Bpv, B_rstd], writes=[B_o[k]])
                    sc.op("dve", lambda v, k=k: v.tensor_tensor(out=xt[:, k, :], in0=xt[:, k, :], in1=ob[:, k, :], op=ALU.add),
                          reads=[B_o[k], B_x[k]], writes=[B_x[k]])

                for k in range(KD):
                    sc.op("act", lambda a_, k=k, p=pvl: a_.activation(
                        out=yb[:, k, :], in_=xt[:, k, :], func=AF.Identity, scale=p[:, 32 + k:33 + k]),
                        reads=[B_x[k], Bpv], writes=[B_y[k]])
                if nx is not None and nx[0] >= 1:
                    phase_a(nx[0], nx[1], 'head')
                    head_done.add(nx)
                st = Stats(KD, 2)
                for k in range(KD):
                    st.add(xt[:, k, :], [B_x[k]])
                st.finish(1.0 / D)
                st = Stats(KD, 2)
                grow = li * D
                for mo in range(KD):
                    a = w_gate[(li * 16 + mo) * 128:(li * 16 + mo + 1) * 128, :]
                    wb, wap = wload(a, ("p (k n) -> p k n", dict(k=KD)))
                    a2 = w_ple[(li * 16 + mo) * 128:(li * 16 + mo + 1) * 128, :]
                    pwb, pwap = wload(a2, ("p (k n) -> p k n", dict(k=2)))
                    gb, gap = R_ps.next()
                    if mo == 0:
                        sc.mm_seq(gb, gap, [(wap[:, k, :], yb[:, k, :], [B_y[k]]) for k in range(KD)], wb)
                    else:
                        sc.mm(gb, gap, [(wap[:, k, :], yb[:, k, :]) for k in range(KD)], reads=wb + B_y[0:KD])
                    eb, eap = R_ps.next()
                    sc.mm(eb, eap, [(pwap[:, k2, :], pb[:, k2, :]) for k2 in range(2)], reads=pwb + [B_pb])
                    tb, tap = R_t.next()
                    sc.op("dve", lambda v, t=tap, g_=gap: v.tensor_tensor(out=t, in0=g_, in1=rstd[:, :], op=ALU.mult),
                          reads=[gb, B_rstd], writes=[tb])
                    tanh_half(tb, tap, tap, [tb])
                    sc.op("dve", lambda v, mo=mo, e=eap, t=tap: v.scalar_tensor_tensor(
                        out=ob[:, mo, :], in0=t, scalar=1.0, in1=e, op0=ALU.add, op1=ALU.mult),
                        reads=[eb, tb], writes=[B_o[mo]])
                    st.add(ob[:, mo, :], [B_o[mo]])
                nxt = None
                if tj + 1 < n_tiles:
                    nxt = li
                elif li + 1 < n_layers:
                    nxt = li + 1

                def tail1(st=st, pvl=pvl, Bpv=Bpv, dst=dst, dst_b=dst_b, tj=tj, c0=c0, last=last):
                    st.finish(0.25 / D, float(np.log(0.5)))
                    for k in range(KD):
                        sc.op("dve", lambda v, k=k, p=pvl: v.scalar_tensor_tensor(
                            out=ob[:, k, :], in0=ob[:, k, :], scalar=p[:, 48 + k:49 + k], in1=rstd[:, :],
                            op0=ALU.mult, op1=ALU.mult), reads=[B_o[k], Bpv, B_rstd], writes=[B_o[k]])
                        sc.op("dve", lambda v, k=k: v.tensor_tensor(out=xt[:, k, :], in0=xt[:, k, :], in1=ob[:, k, :], op=ALU.add),
                              reads=[B_o[k], B_x[k]], writes=[B_x[k]])
                    for q4 in range(4):
                        ks = slice(q4 * 4, q4 * 4 + 4)
                        dap = dst[q4 * 512:(q4 + 1) * 512, c0:c0 + T].rearrange("(k p) t -> p k t", p=128)
                        tok = sc.dma("sp", osem, lambda q, s_=xt[:, ks, :], a=dap: q.dma_start(out=a, in_=s_),
                                     reads=B_x[q4 * 4:q4 * 4 + 4], writes=[dst_b[tj]])
                        if last:
                            sc.final.append((tok[0], tok[1]))

                def tail2(pvl=pvl, Bpv=Bpv, li=li, tj=tj, c0=c0):
                    alias = [b for b, _ in R_ucb.items] + [b for b, _ in R_ucf.items] + [b for b, _ in R_wt.items] \
                        + B_d[0] + B_d[1]
                    sc.op("dve", lambda v: v.memset(fence[:, 0:1], 0.0), writes=alias + [B_fence])
                    for k in range(KD):
                        sc.op("dve", lambda v, k=k, p=pvl: v.scalar_tensor_tensor(
                            out=hst[:, k, :], in0=xt[:, k, :], scalar=p[:, 320 + k:321 + k], in1=rstdb[:, :],
                            op0=ALU.mult, op1=ALU.mult), reads=[B_x[k], Bpv, B_rstdb], writes=[B_hst[k]])

                def tail2b(li=li, tj=tj, c0=c0):
                    hs_stored.add((li, tj))
                    for q4 in range(4):
                        ks = slice(q4 * 4, q4 * 4 + 4)
                        dap = hs[li % 2][q4 * 512:(q4 + 1) * 512, c0:c0 + T].rearrange("(k p) t -> p k t", p=128)
                        sc.dma("sp", hssem, lambda q, s_=hst[:, ks, :], a=dap: q.dma_start(out=a, in_=s_),
                               reads=B_hst[q4 * 4:q4 * 4 + 4], writes=[B_hs[li % 2][tj]])

                if nxt is not None and nxt >= 1:
                    deferred.append((tail1, tail2, not last, tail2b))
                else:
                    tail1()
                    if not last:
                        st2 = Stats(KD, 2)
                        for k in range(KD):
                            st2.add(xt[:, k, :], [B_x[k]])
                        st2.finish(1.0 / D, alt=True)
                        tail2()
                        tail2b()

        sems = {}
        for e in ENGS:
            sems[("E", e)] = es.enter_context(nc.semaphore(f"e_{e}"))
        for ds in wsem + xsem + hlsem + [osem, hssem, psem, vsem]:
            sems[ds.key] = es.enter_context(nc.semaphore("d_" + ds.key[1]))
        block = es.enter_context(nc.Block())
        sc.emit(block, sems)
    return nc


def _vec_pm(v):
    return np.ascontiguousarray(np.asarray(v, np.float32).reshape(-1, 128).T)


def make_inputs(b, x, p, w_in, w_out, g_pre, g_post, pool_w, pool_b, pool_scale,
                conv_w, conv_b, lru_wa, lru_ba, lru_wx, lru_bx, lru_L,
                w_ple, w_ple_gate, g_ple_in, g_ple_out):
    pv = np.zeros((DEPTH, 128, NPV), np.float32)
    for i in range(DEPTH):
        j = i // 2
        pv[i, :, 0:16] = _vec_pm(g_pre[i])
        pv[i, :, 16:32] = _vec_pm(g_post[i])
        pv[i, :, 32:48] = _vec_pm(g_ple_in[i])
        pv[i, :, 48:64] = _vec_pm(g_ple_out[i])
        if i + 1 < DEPTH:
            pv[i, :, 320:336] = _vec_pm(g_pre[i + 1])
        if i % 2 == 0:
            pv[i, :, 64:96] = _vec_pm(pool_b[j])
            pv[i, :, 96:128] = _vec_pm(pool_scale[j])
        else:
            pv[i, :, 64:96] = _vec_pm(conv_b[j])
            pv[i, :, 96:128] = _vec_pm(lru_ba[j])
            pv[i, :, 128:160] = _vec_pm(lru_bx[j])
            pv[i, :, 160:192] = _vec_pm(lru_L[j])
            for k in range(4):
                pv[i, :, 192 + k * 32:192 + (k + 1) * 32] = _vec_pm(conv_w[j, k])
    cst = np.zeros((128, 64), np.float32)
    for g, w in enumerate(POOL_W):
        cst[:, g * 16:(g + 1) * 16] = (1.0 / np.minimum(np.arange(1, 17), w)).astype(np.float32)[None, :]
    f = lambda a: np.ascontiguousarray(np.asarray(a, np.float32))

    def panels(w):
        w = np.asarray(w, np.float32)
        L, K_, M_ = w.shape
        k, m = K_ // 128, M_ // 128
        return np.ascontiguousarray(w.reshape(L, k, 128, m, 128).transpose(0, 3, 2, 1, 4)).reshape(L * m * 128, k * 128)

    def blocks(w):
        w = np.asarray(w, np.float32)
        J = w.shape[0]
        return np.ascontiguousarray(w.reshape(J, 16, 2, 128, 256).transpose(0, 1, 3, 2, 4)).reshape(J * 16 * 128, 512)
    return {
        "xT": f(np.asarray(x[b]).T),
        "pT": f(np.transpose(np.asarray(p[:, b]), (0, 2, 1)).reshape(DEPTH * PLE, S)),
        "w_in": panels(w_in),
        "w_out": panels(w_out),
        "w_gate": panels(w_ple_gate),
        "w_ple": panels(w_ple),
        "pool_w": panels(np.asarray(pool_w).reshape(8, 1024, 1024)),
        "lru_wa": blocks(lru_wa),
        "lru_wx": blocks(lru_wx),
        "pv": pv.reshape(DEPTH * 128, NPV),
        "cst": cst,
    }


N_LAUNCH_CORES = 2


def kernel(**inputs):
    nc = build_program()
    maps = [make_inputs(b, **inputs) for b in range(2)]
    in_maps = [maps[c % 2] for c in range(N_LAUNCH_CORES)]
    res = run_bass_kernel_spmd(nc, in_maps, core_ids=list(range(N_LAUNCH_CORES)))
    out = np.stack([np.ascontiguousarray(res.results[b]["yT"].T) for b in range(2)], axis=0)
    return out.astype(np.float32)
```

```python
import contextlib
import numpy as np
import concourse.bass as bass
import concourse.mybir as mybir
from concourse.bass_utils import run_bass_kernel_spmd

F32 = mybir.dt.float32
BF16 = mybir.dt.bfloat16
AF = mybir.ActivationFunctionType
ALU = mybir.AluOpType

D = 2048
E = 4096
S = 4096
T = 512
NT = S // T
KD = D // 128
KE = E // 128
DEPTH = 4
PLE = 256
EPS = 1e-6
NPV = 336
HALO = 16
UW = HALO + T
NU = 18
UNIT = 1024
NTMP = 8
NUCF = 4
NUB = 3
NPS = 6
POOL_W = (2, 4, 8, 16)

ENGS = ("pe", "act", "dve", "sp", "pool")


class Buf:
    __slots__ = ("lw", "rd")

    def __init__(self):
        self.lw = None
        self.rd = {}


class DSem:
    def __init__(self, key):
        self.key = key
        self.n = 0


class Sched:
    def __init__(self):
        self.streams = {e: [] for e in ENGS}
        self.cnt = {e: 0 for e in ENGS}
        self.seen = {e: {} for e in ENGS}
        self.final = []

    def _waits(self, eng, reads, writes):
        need = {}

        def add(tok, raw):
            key, val, peng, isdma = tok
            if (not isdma) and peng == eng and not raw:
                return
            if need.get(key, 0) < val:
                need[key] = val

        for b in reads:
            if b.lw is not None:
                add(b.lw, True)
        for b in writes:
            if b.lw is not None:
                add(b.lw, False)
            for tok in b.rd.values():
                add(tok, False)
        out = []
        seen = self.seen[eng]
        for key, val in need.items():
            if seen.get(key, 0) < val:
                seen[key] = val
                out.append((key, val))
        return out

    def _commit(self, tok, reads, writes):
        for b in reads:
            old = b.rd.get(tok[0])
            if old is None or old[1] < tok[1]:
                b.rd[tok[0]] = tok
        for b in writes:
            b.lw = tok
            b.rd = {}

    def op(self, eng, fn, reads=(), writes=()):
        waits = self._waits(eng, reads, writes)
        self.cnt[eng] += 1
        key = ("E", eng)
        tok = (key, self.cnt[eng], eng, False)
        self.streams[eng].append((waits, fn, (key, 1)))
        self._commit(tok, reads, writes)
        return tok

    def dma(self, eng, dsem, fn, reads=(), writes=()):
        waits = self._waits(eng, reads, writes)
        dsem.n += 16
        tok = (dsem.key, dsem.n, eng, True)
        self.streams[eng].append((waits, fn, (dsem.key, 16)))
        self._commit(tok, reads, writes)
        return tok

    def mm(self, out_buf, out_ap, pairs, reads):
        waits = self._waits("pe", reads, [out_buf])
        self.cnt["pe"] += 1
        key = ("E", "pe")
        tok = (key, self.cnt["pe"], "pe", False)
        n = len(pairs)
        for i, (l, r) in enumerate(pairs):
            def fn(pe, l=l, r=r, st=(i == 0), sp=(i == n - 1)):
                return pe.matmul(out_ap, l, r, start=st, stop=sp)
            self.streams["pe"].append((waits if i == 0 else [], fn, (key, 1) if i == n - 1 else None))
        self._commit(tok, reads, [out_buf])
        return tok

    def mm_seq(self, out_buf, out_ap, items, common):
        n = len(items)
        key = ("E", "pe")
        allr = list(common)
        for i, (l, r, rb) in enumerate(items):
            rds = list(rb) + (list(common) if i == 0 else [])
            waits = self._waits("pe", rds, [out_buf] if i == 0 else [])
            allr += list(rb)

            def fn(pe, l=l, r=r, st=(i == 0), sp=(i == n - 1)):
                return pe.matmul(out_ap, l, r, start=st, stop=sp)
            self.streams["pe"].append((waits, fn, (key, 1) if i == n - 1 else None))
        self.cnt["pe"] += 1
        tok = (key, self.cnt["pe"], "pe", False)
        self._commit(tok, allr, [out_buf])
        return tok

    def mm1(self, out_buf, out_ap, l, r, start, stop, reads):
        writes = [out_buf] if start else []
        rds = list(reads)
        waits = self._waits("pe", rds, writes)
        self.cnt["pe"] += 1
        key = ("E", "pe")
        tok = (key, self.cnt["pe"], "pe", False)

        def fn(pe):
            return pe.matmul(out_ap, l, r, start=start, stop=stop)
        self.streams["pe"].append((waits, fn, (key, 1)))
        if start:
            self._commit(tok, rds, [out_buf])
        else:
            self._commit(tok, rds, [])
            out_buf.lw = tok
        return tok

    def emit(self, block, sems):
        def make(name):
            def body(eng):
                for waits, fn, inc in self.streams[name]:
                    for key, val in waits:
                        eng.wait_ge(sems[key], val)
                    ins = fn(eng)
                    if inc is not None:
                        ins.then_inc(sems[inc[0]], inc[1])
                if name == "sp":
                    for key, val in self.final:
                        eng.wait_ge(sems[key], val)
            return body
        block.tensor(make("pe"))
        block.scalar(make("act"))
        block.vector(make("dve"))
        block.sync(make("sp"))
        block.gpsimd(make("pool"))


class Rot:
    def __init__(self, aps):
        self.items = [(Buf(), ap) for ap in aps]
        self.i = 0

    def next(self):
        it = self.items[self.i % len(self.items)]
        self.i += 1
        return it


def build_program(n_layers=DEPTH, n_tiles=NT):
    nc = bass.Bass("TRN2", target_bir_lowering=False)
    sc = Sched()

    def din(name, shape):
        return nc.dram_tensor(name, shape, F32, kind="ExternalInput").ap()

    xT = din("xT", [D, S])
    pT = din("pT", [DEPTH * PLE, S])
    w_in = din("w_in", [DEPTH * 64 * 128, D])
    w_out = din("w_out", [DEPTH * 16 * 128, E])
    w_gate = din("w_gate", [DEPTH * 16 * 128, D])
    w_ple = din("w_ple", [DEPTH * 16 * 128, PLE])
    pool_w = din("pool_w", [2 * 4 * 8 * 128, 1024])
    lru_wa = din("lru_wa", [2 * 16 * 128, 512])
    lru_wx = din("lru_wx", [2 * 16 * 128, 512])
    pv_d = din("pv", [DEPTH * 128, NPV])
    cst_d = din("cst", [128, 64])
    yT = nc.dram_tensor("yT", [D, S], F32, kind="ExternalOutput").ap()
    xs = [nc.dram_tensor("xs0", [D, S], F32).ap(), nc.dram_tensor("xs1", [D, S], F32).ap()]
    hs = [nc.dram_tensor("hs0", [D, S], BF16).ap(), nc.dram_tensor("hs1", [D, S], BF16).ap()]

    es = contextlib.ExitStack()
    with es:
        def sb(name, shape, dt):
            return es.enter_context(nc.sbuf_tensor(name, shape, dt))

        xt = sb("xt", [128, KD, T], F32)
        hb = sb("hb", [128, KD, T], BF16)
        yb = sb("yb", [128, KE, T], BF16)
        ob = sb("ob", [128, KD, T], F32)
        obh = ob[:, :, :].bitcast(BF16)
        wsl = sb("wsl", [128, NU * UNIT], BF16)
        ubuf = sb("ubuf", [128, NUB, UW], F32)
        mixf = sb("mixf", [128, 4096 + 2 * UW], F32)
        dbuf = mixf[:, 0:4096].bitcast(BF16).rearrange("p (a b t) -> p a b t", a=2, b=8)
        wtmp = mixf[:, 4096:4096 + 2 * UW].rearrange("p (a t) -> p a t", a=2)
        ftmp = sb("ftmp", [128, NTMP, T], F32)
        hst = mixf[:, 0:4096].bitcast(BF16).rearrange("p (k t) -> p k t", k=KD)
        ucft = mixf[:, 0:NUCF * T].rearrange("p (a t) -> p a t", a=NUCF)
        sqb = sb("sqb", [128, 4, T], BF16)
        ucb = mixf[:, NUCF * T:NUCF * T + 2 * T].bitcast(BF16).rearrange("p (a t) -> p a t", a=4)
        rstd = sb("rstd", [128, T], F32)
        rstdb = sb("rstdb", [128, T], F32)
        pb = sb("pb", [128, 2, T], BF16)
        pvt = sb("pvt", [128, 2, NPV], F32)
        dvt = sb("dvt", [128, 2, 128], F32)
        carry = sb("carry", [128, KE, HALO], F32)
        state = sb("state", [128, KE], F32)
        ones = sb("ones", [128, 128], BF16)
        cst = sb("cst_sb", [128, 64], F32)
        fence = sb("fence", [128, 8], F32)
        psums = [es.enter_context(nc.psum_tensor(f"ps{i}", [128, T], F32)) for i in range(8)]

        B_x = [Buf() for _ in range(KD)]
        B_h = [Buf() for _ in range(KD)]
        B_y = [Buf() for _ in range(KE)]
        B_o = [Buf() for _ in range(KD)]
        B_w = [Buf() for _ in range(NU)]
        B_d = [[Buf() for _ in range(8)] for _ in range(2)]
        B_rstd = Buf()
        B_rstdb = Buf()
        B_hst = [Buf() for _ in range(KD)]
        B_pb = Buf()
        B_pv = [Buf(), Buf()]
        B_dv = [Buf(), Buf()]
        B_carry = [Buf() for _ in range(KE)]
        B_state = [Buf() for _ in range(KE)]
        B_ones = Buf()
        B_cst = Buf()
        B_fence = Buf()
        B_xs = [[Buf() for _ in range(n_tiles)] for _ in range(2)]
        B_hs = [[Buf() for _ in range(n_tiles)] for _ in range(2)]
        B_out = [Buf() for _ in range(n_tiles)]
        B_in = Buf()
        R_ps = Rot([psums[i][:, :] for i in range(NPS)])
        R_ss = Rot([psums[NPS + i][:, :] for i in range(2)])
        R_u = Rot([ubuf[:, i, :] for i in range(NUB)])
        R_wt = Rot([wtmp[:, i, :] for i in range(2)])
        R_t = Rot([ftmp[:, i, :] for i in range(NTMP)])
        R_ucf = Rot([ucft[:, i, :] for i in range(NUCF)])
        R_sq = Rot([sqb[:, i, :] for i in range(4)])
        R_ucb = Rot([ucb[:, i, :] for i in range(4)])

        wsem = [DSem(("D", f"w{i}")) for i in range(NU)]
        xsem = [DSem(("D", f"x{i}")) for i in range(4)]
        osem = DSem(("D", "o"))
        hssem = DSem(("D", "hs"))
        hlsem = [DSem(("D", f"hl{i}")) for i in range(4)]
        deferred = []
        psem = DSem(("D", "p"))
        vsem = DSem(("D", "v"))
        wctr = [0]

        def wload(src_ap, view):
            n = 1
            for dsz in src_ap.shape[1:]:
                n *= dsz
            nun = (n + UNIT - 1) // UNIT
            if wctr[0] + nun > NU:
                wctr[0] = 0
            s = wctr[0]
            wctr[0] += nun
            dst = wsl[:, s * UNIT:s * UNIT + n]
            if view is not None:
                dst = dst.rearrange(view[0], **view[1])
            bufs = B_w[s:s + nun]
            sc.dma("pool", wsem[s], lambda g, d=dst, a=src_ap: g.dma_start(out=d, in_=a),
                   reads=[B_in], writes=bufs)
            return bufs, dst

        def act(out, in_, func, reads, writes, bias=0.0, scale=1.0):
            sc.op("act", lambda a: a.activation(out=out, in_=in_, func=func, bias=bias, scale=scale),
                  reads=reads, writes=writes)

        def tanh_half(tb, tap, src_ap, src_bufs, hbias=None):
            if hbias is None:
                act(tap, src_ap, AF.Tanh, src_bufs, [tb], scale=0.5)
            else:
                act(tap, src_ap, AF.Tanh, src_bufs + [hbias[0]], [tb], bias=hbias[1], scale=0.5)

        class Stats:
            def __init__(self, n, lag):
                self.n, self.lag, self.i, self.pend = n, lag, 0, []
                self.ssb, self.ssap = R_ss.next()

            def add(self, src_ap, src_bufs):
                qb, qap = R_sq.next()
                act(qap, src_ap, AF.Square, src_bufs, [qb])
                self.pend.append((qb, qap))
                while len(self.pend) > self.lag:
                    self._one()

            def _one(self):
                qb, qap = self.pend.pop(0)
                sc.mm1(self.ssb, self.ssap, ones[:, :], qap, self.i == 0, self.i == self.n - 1, [qb, B_ones])
                self.i += 1

            def finish(self, mean_scale, lnbias=0.0, alt=False):
                while self.pend:
                    self._one()
                finish_rstd(self.ssb, self.ssap, mean_scale, lnbias, alt)

        def finish_rstd(ssb, ssap, mean_scale, lnbias=0.0, alt=False):
            tb, tap = R_t.next()
            act(tap, ssap, AF.Ln, [ssb], [tb], bias=EPS, scale=mean_scale)
            if alt:
                act(rstdb[:, :], tap, AF.Exp, [tb], [B_rstdb], bias=lnbias, scale=-0.5)
            else:
                act(rstd[:, :], tap, AF.Exp, [tb], [B_rstd], bias=lnbias, scale=-0.5)

        sc.op("dve", lambda v: v.memset(ones[:, :], 1.0), writes=[B_ones])
        sc.dma("sp", vsem, lambda q: q.dma_start(out=cst[:, :], in_=cst_d[:, :]), reads=[B_in], writes=[B_cst])

        pends = {}
        head_done = set()
        h_done = set()
        hs_stored = set()
        setup_done = set()

        def layer_ctx(li):
            par = li % 2
            return (li % 2 == 0, li // 2, pvt[:, par, :], dvt[:, par, :], B_pv[par], B_dv[par],
                    xT if li == 0 else xs[(li - 1) % 2], None if li == 0 else B_xs[(li - 1) % 2])

        def layer_setup(li):
            if li in setup_done:
                return
            setup_done.add(li)
            is_pool, jj, pvl, dvl, Bpv, Bdv, src, src_b = layer_ctx(li)
            sc.dma("sp", vsem, lambda q, d=pvl, a=pv_d[li * 128:(li + 1) * 128, :]: q.dma_start(out=d, in_=a),
                   reads=[B_in], writes=[Bpv])
            if is_pool:
                sc.op("dve", lambda v, d=dvl, p=pvl: v.tensor_tensor(out=d[:, 0:32], in0=p[:, 64:96], in1=p[:, 96:128], op=ALU.mult),
                      reads=[Bpv], writes=[Bdv])
            else:
                sc.op("dve", lambda v, d=dvl, p=pvl: v.tensor_scalar(out=d[:, 0:64], in0=p[:, 96:160], scalar1=0.5, scalar2=None, op0=ALU.mult),
                      reads=[Bpv], writes=[Bdv])
                act(dvl[:, 64:96], pvl[:, 160:192], AF.Exp, [Bpv], [Bdv], scale=-1.0)
                act(dvl[:, 64:96], dvl[:, 64:96], AF.Ln, [Bdv], [Bdv], bias=1.0)
                sc.op("dve", lambda v, d=dvl: v.tensor_scalar(out=d[:, 96:128], in0=d[:, 64:96], scalar1=-4.0, scalar2=None, op0=ALU.mult),
                      reads=[Bdv], writes=[Bdv])
                sc.op("dve", lambda v, d=dvl: v.tensor_scalar(out=d[:, 64:96], in0=d[:, 64:96], scalar1=-8.0, scalar2=None, op0=ALU.mult),
                      reads=[Bdv], writes=[Bdv])
            sc.op("dve", lambda v: v.memset(carry[:, :, :], 0.0), writes=B_carry)
            sc.op("dve", lambda v: v.memset(state[:, :], 0.0), writes=B_state)


        def make_load_x(li, tj):
            is_pool, jj, pvl, dvl, Bpv, Bdv, src, src_b = layer_ctx(li)
            c0 = tj * T

            def load_x(tj=tj, c0=c0):
                for q4 in range(4):
                    ks = slice(q4 * 4, q4 * 4 + 4)
                    sap = src[q4 * 512:(q4 + 1) * 512, c0:c0 + T].rearrange("(k p) t -> p k t", p=128)
                    rd = [B_in] if src_b is None else [src_b[tj]]
                    sc.dma("sp", xsem[q4], lambda q, d=xt[:, ks, :], a=sap: q.dma_start(out=d, in_=a),
                           reads=rd, writes=B_x[q4 * 4:q4 * 4 + 4])

            return load_x

        def phase_a(li, tj, part):
            is_pool, jj, pvl, dvl, Bpv, Bdv, src, src_b = layer_ctx(li)
            c0 = tj * T
            use_hload = (li >= 1)
            load_x = make_load_x(li, tj)
            if part in ('head', 'all'):
                sc.op("dve", lambda v: v.memset(fence[:, 0:1], 0.0), writes=B_hst + [B_fence])
            if part in ('h', 'all'):
                if not use_hload:
                    load_x()
                    st = Stats(KD, 2)
                    for k in range(KD):
                        st.add(xt[:, k, :], [B_x[k]])
                    st.finish(1.0 / D)
                    for k in range(KD):
                        sc.op("dve", lambda v, k=k, p=pvl: v.scalar_tensor_tensor(
                            out=hb[:, k, :], in0=xt[:, k, :], scalar=p[:, k:k + 1], in1=rstd[:, :],
                            op0=ALU.mult, op1=ALU.mult), reads=[B_x[k], Bpv, B_rstd], writes=[B_h[k]])
                else:
                    for q4 in range(4):
                        ks = slice(q4 * 4, q4 * 4 + 4)
                        sap = hs[(li - 1) % 2][q4 * 512:(q4 + 1) * 512, c0:c0 + T].rearrange("(k p) t -> p k t", p=128)
                        sc.dma("sp", hlsem[q4], lambda q, d=hb[:, ks, :], a=sap: q.dma_start(out=d, in_=a),
                               reads=[B_hs[(li - 1) % 2][tj]], writes=B_h[q4 * 4:q4 * 4 + 4])

            wrow = li * D

            def in_chunk(c):
                a = w_in[(li * 64 + c) * 128:(li * 64 + c + 1) * 128, :]
                return wload(a, ("p (k n) -> p k n", dict(k=KD)))

            def proj_in(c):
                wb, wap = in_chunk(c)
                pbuf, pap_ = R_ps.next()
                sc.mm(pbuf, pap_, [(wap[:, k, :], hb[:, k, :]) for k in range(KD)], reads=wb + B_h)
                return pbuf, pap_

            def z_finish(m, zb, zap, tb, tap, hsb, hsap):
                sc.op("dve", lambda v, t=tap, h=hsap, m=m: v.tensor_tensor(
                    out=yb[:, m, :], in0=h, in1=t, op=ALU.mult), reads=[tb, hsb], writes=[B_y[m]])

            def z_to_y(m, zb, zap, hsb, hsap):
                tb, tap = R_t.next()
                act(tap, zap, AF.Silu, [zb], [tb])
                z_finish(m, zb, zap, tb, tap, hsb, hsap)

            if is_pool:
                def u_group(g):
                    w = POOL_W[g]
                    for half_pair in range(4):
                        for half in range(2):
                            m = g * 8 + half_pair * 2 + half
                            pbuf, pap_ = proj_in(m)
                            ub, uap = R_u.next()
                            act(uap[:, HALO:UW], pap_, AF.Copy, [pbuf], [ub])
                            sc.op("dve", lambda v, u=uap, m=m: v.tensor_copy(out=u[:, 0:HALO], in_=carry[:, m, :]),
                                  reads=[B_carry[m]], writes=[ub])
                            cur_b, cur = ub, uap
                            lo = -HALO
                            step = 1
                            while step < w:
                                nb, nap = R_wt.next()
                                nlo = lo + step
                                a0 = HALO + nlo
                                sc.op("dve", lambda v, o=nap, c=cur, a0=a0, s=step: v.tensor_tensor(
                                    out=o[:, a0:UW], in0=c[:, a0:UW], in1=c[:, a0 - s:UW - s], op=ALU.add),
                                    reads=[cur_b], writes=[nb])
                                cur_b, cur, lo = nb, nap, nlo
                                step *= 2
                            db = B_d[g % 2][m % 8]
                            dap = dbuf[:, g % 2, m % 8, :]
                            sc.op("dve", lambda v, o=dap, c=cur, u=uap, w=w: v.scalar_tensor_tensor(
                                out=o, in0=c[:, HALO:UW], scalar=1.0 / w, in1=u[:, HALO:UW],
                                op0=ALU.mult, op1=ALU.subtract), reads=[cur_b, ub], writes=[db])
                            if tj == 0:
                                tb, tap = R_t.next()
                                sc.op("dve", lambda v, t=tap, c=cur, g=g: v.tensor_tensor(
                                    out=t[:, 0:16], in0=c[:, HALO:HALO + 16], in1=cst[:, g * 16:(g + 1) * 16], op=ALU.mult),
                                    reads=[cur_b, B_cst], writes=[tb])
                                sc.op("dve", lambda v, o=dap, t=tap, u=uap: v.tensor_tensor(
                                    out=o[:, 0:16], in0=t[:, 0:16], in1=u[:, HALO:HALO + 16], op=ALU.subtract),
                                    reads=[tb, ub], writes=[db])
                            sc.op("dve", lambda v, u=uap, m=m: v.tensor_copy(out=carry[:, m, :], in_=u[:, T:UW]),
                                  reads=[ub], writes=[B_carry[m]])

                def pz_group(g):
                    prow = (jj * 4 + g) * 1024
                    for hf in range(1):
                        for pr in range(1):
                            for mo in range(8):
                                m = g * 8 + mo
                                a = pool_w[((jj * 4 + g) * 8 + mo) * 128:((jj * 4 + g) * 8 + mo + 1) * 128, :]
                                pwb, pwap = wload(a, ("p (k n) -> p k n", dict(k=8)))
                                zb, zap = proj_in(KE + m)
                                ybf, yap = R_ps.next()
                                sc.mm(ybf, yap, [(pwap[:, ki, :], dbuf[:, g % 2, ki, :]) for ki in range(8)],
                                      reads=pwb + B_d[g % 2])
                                hsb, hsap = R_t.next()
                                sc.op("act", lambda a_, o=hsap, i=yap, m=m, p=pvl, d=dvl: a_.activation(
                                    out=o, in_=i, func=AF.Identity, bias=d[:, m:m + 1], scale=p[:, 96 + m:97 + m]),
                                    reads=[ybf, Bpv, Bdv], writes=[hsb])
                                z_to_y(m, zb, zap, hsb, hsap)

                if part in ('head', 'all'):
                    u_group(0)
                if part in ('rest', 'all'):
                    for g in range(1, 4):
                        u_group(g)
                        pz_group(g - 1)
                    pz_group(3)
            else:
                pend = pends.setdefault((li, tj), {})

                def u_block(q):
                    res = []
                    for half in range(2):
                        m = 2 * q + half
                        pbuf, pap_ = proj_in(m)
                        ub, uap = R_u.next()
                        act(uap[:, HALO:UW], pap_, AF.Copy, [pbuf], [ub])
                        sc.op("dve", lambda v, u=uap, m=m: v.tensor_copy(out=u[:, HALO - 3:HALO], in_=carry[:, m, HALO - 3:HALO]),
                              reads=[B_carry[m]], writes=[ub])
                        ucf_b, ucf = R_ucf.next()
                        cw = 192
                        sc.op("dve", lambda v, o=ucf, u=uap, m=m, p=pvl: v.tensor_scalar(
                            out=o, in0=u[:, HALO - 3:UW - 3], scalar1=p[:, 192 + m:193 + m], scalar2=p[:, 64 + m:65 + m],
                            op0=ALU.mult, op1=ALU.add), reads=[ub, Bpv], writes=[ucf_b])
                        for kk in range(1, 4):
                            sc.op("dve", lambda v, o=ucf, u=uap, m=m, kk=kk, p=pvl: v.scalar_tensor_tensor(
                                out=o, in0=u[:, HALO - 3 + kk:UW - 3 + kk], scalar=p[:, cw + kk * 32 + m:cw + kk * 32 + m + 1],
                                in1=o, op0=ALU.mult, op1=ALU.add), reads=[ub, Bpv, ucf_b], writes=[ucf_b])
                        cb_, cap = R_ucb.next()
                        sc.op("dve", lambda v, o=cap, i=ucf: v.tensor_copy(out=o, in_=i), reads=[ucf_b], writes=[cb_])
                        sc.op("dve", lambda v, u=uap, m=m: v.tensor_copy(out=carry[:, m, HALO - 3:HALO], in_=u[:, UW - 3:UW]),
                              reads=[ub], writes=[B_carry[m]])
                        res.append((m, ucf_b, ucf, cb_, cap))
                    pend[q] = res

                gate_w = {}

                def g_block(q):
                    r0 = (jj * 16 + q) * 128
                    aa = lru_wa[r0:r0 + 128, :]
                    ax = lru_wx[r0:r0 + 128, :]
                    wab, waap = wload(aa, ("p (k n) -> p k n", dict(k=2)))
                    wxb, wxap = wload(ax, ("p (k n) -> p k n", dict(k=2)))
                    res = pend.pop(q)
                    cs = []
                    for jo in range(2):
                        m = res[jo][0]
                        rb, rap = R_ps.next()
                        sc.mm(rb, rap, [(waap[:, ii, jo * 128:(jo + 1) * 128], res[ii][4]) for ii in range(2)],
                              reads=wab + [res[0][3], res[1][3]])
                        ib, iap = R_ps.next()
                        sc.mm(ib, iap, [(wxap[:, ii, jo * 128:(jo + 1) * 128], res[ii][4]) for ii in range(2)],
                              reads=wxb + [res[0][3], res[1][3]])
                        zb, zap = proj_in(KE + m)
                        cs.append((rb, rap, ib, iap, zb, zap))
                    tm = []
                    for jo in range(2):
                        m = res[jo][0]
                        rb, rap, ib, iap, zb, zap = cs[jo]
                        t1b, t1 = R_t.next()
                        tab, ta = R_t.next()
                        t2b, t2 = R_t.next()
                        t3b, t3 = R_t.next()
                        tanh_half(t1b, t1, rap, [rb], hbias=(Bdv, dvl[:, m:m + 1]))
                        tanh_half(t2b, t2, iap, [ib], hbias=(Bdv, dvl[:, 32 + m:33 + m]))
                        act(t3, zap, AF.Silu, [zb], [t3b])
                        tm.append((t1b, t1, tab, ta, t2b, t2, t3b, t3))
                    for jo in range(2):
                        m = res[jo][0]
                        t1b, t1, tab, ta, t2b, t2, t3b, t3 = tm[jo]
                        sc.op("act", lambda a_, o=ta, i=t1, m=m, d=dvl: a_.activation(
                            out=o, in_=i, func=AF.Exp, bias=d[:, 96 + m:97 + m], scale=d[:, 96 + m:97 + m]),
                            reads=[t1b, Bdv], writes=[tab])
                        sc.op("act", lambda a_, o=t1, i=t1, m=m, d=dvl: a_.activation(
                            out=o, in_=i, func=AF.Exp, bias=d[:, 64 + m:65 + m], scale=d[:, 64 + m:65 + m]),
                            reads=[t1b, Bdv], writes=[t1b])
                        act(t1, t1, AF.Ln, [t1b], [t1b], bias=1.0, scale=-1.0)
                        act(t1, t1, AF.Exp, [t1b], [t1b], scale=0.5)
                    for jo in range(2):
                        m, ucf_b, ucf, _, _ = res[jo]
                        rb, rap, ib, iap, zb, zap = cs[jo]
                        t1b, t1, tab, ta, t2b, t2, t3b, t3 = tm[jo]
                        sc.op("dve", lambda v, a=t2, b=ucf: v.scalar_tensor_tensor(
                            out=a, in0=a, scalar=1.0, in1=b, op0=ALU.add, op1=ALU.mult),
                            reads=[t2b, ucf_b], writes=[t2b])
                        sc.op("dve", lambda v, a=t2, b=t1: v.scalar_tensor_tensor(
                            out=a, in0=a, scalar=0.5, in1=b, op0=ALU.mult, op1=ALU.mult),
                            reads=[t1b, t2b], writes=[t2b])
                        sc.op("dve", lambda v, o=t1, a=ta, b=t2, m=m: v.tensor_tensor_scan(
                            out=o, data0=a, data1=b, initial=state[:, m:m + 1], op0=ALU.mult, op1=ALU.add),
                            reads=[tab, t2b, B_state[m]], writes=[t1b])
                        sc.op("dve", lambda v, o=t1, m=m: v.tensor_copy(out=state[:, m:m + 1], in_=o[:, T - 1:T]),
                              reads=[t1b], writes=[B_state[m]])
                        z_finish(m, zb, zap, t3b, t3, t1b, t1)

                if part in ('head', 'all'):
                    u_block(0)
                    u_block(1)
                if part in ('rest', 'all'):
                    g_block(0)
                    for q in range(2, 16):
                        u_block(q)
                        g_block(q - 1)
                    g_block(15)


        for li in range(n_layers):
            is_pool = (li % 2 == 0)
            jj = li // 2
            par = li % 2
            src = xT if li == 0 else xs[(li - 1) % 2]
            src_b = None if li == 0 else B_xs[(li - 1) % 2]
            last = (li == n_layers - 1)
            dst = yT if last else xs[li % 2]
            dst_b = B_out if last else B_xs[li % 2]
            pvl = pvt[:, par, :]
            dvl = dvt[:, par, :]
            Bpv = B_pv[par]
            Bdv = B_dv[par]

            layer_setup(li)

            for tj in range(n_tiles):
                c0 = tj * T
                use_hload = (li >= 1)
                load_x = make_load_x(li, tj)
                if (li, tj) not in h_done:
                    phase_a(li, tj, 'h')
                if (li, tj) not in head_done:
                    phase_a(li, tj, 'head')
                phase_a(li, tj, 'rest')
                if tj + 1 < n_tiles:
                    nx = (li, tj + 1)
                elif li + 1 < n_layers:
                    nx = (li + 1, 0)
                else:
                    nx = None

                pap = pT[li * PLE:(li + 1) * PLE, c0:c0 + T].rearrange("(k p) t -> p k t", p=128)
                sc.dma("pool", psem, lambda g, a=pap: g.dma_start(out=pb[:, :, :], in_=a), reads=[B_in], writes=[B_pb])

                def next_h():
                    if nx is not None and nx[0] >= 1 and nx not in h_done:
                        layer_setup(nx[0])
                        phase_a(nx[0], nx[1], 'h')
                        h_done.add(nx)

                prev = deferred.pop(0) if deferred else None
                st2 = None
                if prev is not None:
                    prev[0]()
                    if prev[2]:
                        st2 = Stats(KD, 1)
                st = Stats(KD, 1 if st2 is not None else 2)
                if st2 is None or (nx is not None and (nx[0] - 1, nx[1]) in hs_stored):
                    next_h()
                orow = li * E
                x_loaded = not use_hload
                for mo in range(KD):
                    a = w_out[(li * 16 + mo) * 128:(li * 16 + mo + 1) * 128, :]
                    wb, wap = wload(a, ("p (k n) -> p k n", dict(k=KE)))
                    pbuf, pap_ = R_ps.next()
                    if mo == 0:
                        sc.mm_seq(pbuf, pap_, [(wap[:, k, :], yb[:, k, :], [B_y[k]]) for k in range(KE)], wb)
                    else:
                        sc.mm(pbuf, pap_, [(wap[:, k, :], yb[:, k, :]) for k in range(KE)], reads=wb + B_y)
                    act(ob[:, mo, :], pap_, AF.Copy, [pbuf], [B_o[mo]])
                    st.add(pap_, [pbuf])
                    if st2 is not None:
                        if 2 <= mo < 10:
                            st2.add(xt[:, 2 * (mo - 2), :], [B_x[2 * (mo - 2)]])
                            st2.add(xt[:, 2 * (mo - 2) + 1, :], [B_x[2 * (mo - 2) + 1]])
                        elif mo == 10:
                            st2.finish(1.0 / D, alt=True)
                            prev[1]()
                            if not x_loaded:
                                load_x()
                                x_loaded = True
                            prev[3]()
                            next_h()
                    if not x_loaded and (prev is None or (st2 is None and mo == 3)):
                        load_x()
                        x_loaded = True
                st.finish(1.0 / D)
                for k in range(KD):
                    sc.op("dve", lambda v, k=k, p=pvl: v.scalar_tensor_tensor(
                        out=ob[:, k, :], in0=ob[:, k, :], scalar=p[:, 16 + k:17 + k], in1=rstd[:, :],
                        op0=ALU.mult, op1=ALU.mult), reads=[B_o[k], Bpv, B_rstd], writes=[B_o[k]])
                    sc.op("dve", lambda v, k=k: v.tensor_tensor(out=xt[:, k, :], in0=xt[:, k, :], in1=ob[:, k, :], op=ALU.add),
                          reads=[B_o[k], B_x[k]], writes=[B_x[k]])

                for k in range(KD):
                    sc.op("act", lambda a_, k=k, p=pvl: a_.activation(
                        out=yb[:, k, :], in_=xt[:, k, :], func=AF.Identity, scale=p[:, 32 + k:33 + k]),
                        reads=[B_x[k], Bpv], writes=[B_y[k]])
                if nx is not None and nx[0] >= 1:
                    phase_a(nx[0], nx[1], 'head')
                    head_done.add(nx)
                st = Stats(KD, 2)
                for k in range(KD):
                    st.add(xt[:, k, :], [B_x[k]])
                st.finish(1.0 / D)
                st = Stats(KD, 2)
                grow = li * D
                for mo in range(KD):
                    a = w_gate[(li * 16 + mo) * 128:(li * 16 + mo + 1) * 128, :]
                    wb, wap = wload(a, ("p (k n) -> p k n", dict(k=KD)))
                    a2 = w_ple[(li * 16 + mo) * 128:(li * 16 + mo + 1) * 128, :]
                    pwb, pwap = wload(a2, ("p (k n) -> p k n", dict(k=2)))
                    gb, gap = R_ps.next()
                    if mo == 0:
                        sc.mm_seq(gb, gap, [(wap[:, k, :], yb[:, k, :], [B_y[k]]) for k in range(KD)], wb)
                    else:
                        sc.mm(gb, gap, [(wap[:, k, :], yb[:, k, :]) for k in range(KD)], reads=wb + B_y[0:KD])
                    eb, eap = R_ps.next()
                    sc.mm(eb, eap, [(pwap[:, k2, :], pb[:, k2, :]) for k2 in range(2)], reads=pwb + [B_pb])
                    tb, tap = R_t.next()
                    sc.op("dve", lambda v, t=tap, g_=gap: v.tensor_tensor(out=t, in0=g_, in1=rstd[:, :], op=ALU.mult),
                          reads=[gb, B_rstd], writes=[tb])
                    tanh_half(tb, tap, tap, [tb])
                    sc.op("dve", lambda v, mo=mo, e=eap, t=tap: v.scalar_tensor_tensor(
                        out=ob[:, mo, :], in0=t, scalar=1.0, in1=e, op0=ALU.add, op1=ALU.mult),
                        reads=[eb, tb], writes=[B_o[mo]])
                    st.add(ob[:, mo, :], [B_o[mo]])
                nxt = None
                if tj + 1 < n_tiles:
                    nxt = li
                elif li + 1 < n_layers:
                    nxt = li + 1

                def tail1(st=st, pvl=pvl, Bpv=Bpv, dst=dst, dst_b=dst_b, tj=tj, c0=c0, last=last):
                    st.finish(0.25 / D, float(np.log(0.5)))
                    for k in range(KD):
                        sc.op("dve", lambda v, k=k, p=pvl: v.scalar_tensor_tensor(
                            out=ob[:, k, :], in0=ob[:, k, :], scalar=p[:, 48 + k:49 + k], in1=rstd[:, :],
                            op0=ALU.mult, op1=ALU.mult), reads=[B_o[k], Bpv, B_rstd], writes=[B_o[k]])
                        sc.op("dve", lambda v, k=k: v.tensor_tensor(out=xt[:, k, :], in0=xt[:, k, :], in1=ob[:, k, :], op=ALU.add),
                              reads=[B_o[k], B_x[k]], writes=[B_x[k]])
                    for q4 in range(4):
                        ks = slice(q4 * 4, q4 * 4 + 4)
                        dap = dst[q4 * 512:(q4 + 1) * 512, c0:c0 + T].rearrange("(k p) t -> p k t", p=128)
                        tok = sc.dma("sp", osem, lambda q, s_=xt[:, ks, :], a=dap: q.dma_start(out=a, in_=s_),
                                     reads=B_x[q4 * 4:q4 * 4 + 4], writes=[dst_b[tj]])
                        if last:
                            sc.final.append((tok[0], tok[1]))

                def tail2(pvl=pvl, Bpv=Bpv, li=li, tj=tj, c0=c0):
                    alias = [b for b, _ in R_ucb.items] + [b for b, _ in R_ucf.items] + [b for b, _ in R_wt.items] \
                        + B_d[0] + B_d[1]
                    sc.op("dve", lambda v: v.memset(fence[:, 0:1], 0.0), writes=alias + [B_fence])
                    for k in range(KD):
                        sc.op("dve", lambda v, k=k, p=pvl: v.scalar_tensor_tensor(
                            out=hst[:, k, :], in0=xt[:, k, :], scalar=p[:, 320 + k:321 + k], in1=rstdb[:, :],
                            op0=ALU.mult, op1=ALU.mult), reads=[B_x[k], Bpv, B_rstdb], writes=[B_hst[k]])

                def tail2b(li=li, tj=tj, c0=c0):
                    hs_stored.add((li, tj))
                    for q4 in range(4):
                        ks = slice(q4 * 4, q4 * 4 + 4)
                        dap = hs[li % 2][q4 * 512:(q4 + 1) * 512, c0:c0 + T].rearrange("(k p) t -> p k t", p=128)
                        sc.dma("sp", hssem, lambda q, s_=hst[:, ks, :], a=dap: q.dma_start(out=a, in_=s_),
                               reads=B_hst[q4 * 4:q4 * 4 + 4], writes=[B_hs[li % 2][tj]])

                if nxt is not None and nxt >= 1:
                    deferred.append((tail1, tail2, not last, tail2b))
                else:
                    tail1()
                    if not last:
                        st2 = Stats(KD, 2)
                        for k in range(KD):
                            st2.add(xt[:, k, :], [B_x[k]])
                        st2.finish(1.0 / D, alt=True)
                        tail2()
                        tail2b()

        sems = {}
        for e in ENGS:
            sems[("E", e)] = es.enter_context(nc.semaphore(f"e_{e}"))
        for ds in wsem + xsem + hlsem + [osem, hssem, psem, vsem]:
            sems[ds.key] = es.enter_context(nc.semaphore("d_" + ds.key[1]))
        block = es.enter_context(nc.Block())
        sc.emit(block, sems)
    return nc


def _vec_pm(v):
    return np.ascontiguousarray(np.asarray(v, np.float32).reshape(-1, 128).T)


def make_inputs(b, x, p, w_in, w_out, g_pre, g_post, pool_w, pool_b, pool_scale,
                conv_w, conv_b, lru_wa, lru_ba, lru_wx, lru_bx, lru_L,
                w_ple, w_ple_gate, g_ple_in, g_ple_out):
    pv = np.zeros((DEPTH, 128, NPV), np.float32)
    for i in range(DEPTH):
        j = i // 2
        pv[i, :, 0:16] = _vec_pm(g_pre[i])
        pv[i, :, 16:32] = _vec_pm(g_post[i])
        pv[i, :, 32:48] = _vec_pm(g_ple_in[i])
        pv[i, :, 48:64] = _vec_pm(g_ple_out[i])
        if i + 1 < DEPTH:
            pv[i, :, 320:336] = _vec_pm(g_pre[i + 1])
        if i % 2 == 0:
            pv[i, :, 64:96] = _vec_pm(pool_b[j])
            pv[i, :, 96:128] = _vec_pm(pool_scale[j])
        else:
            pv[i, :, 64:96] = _vec_pm(conv_b[j])
            pv[i, :, 96:128] = _vec_pm(lru_ba[j])
            pv[i, :, 128:160] = _vec_pm(lru_bx[j])
            pv[i, :, 160:192] = _vec_pm(lru_L[j])
            for k in range(4):
                pv[i, :, 192 + k * 32:192 + (k + 1) * 32] = _vec_pm(conv_w[j, k])
    cst = np.zeros((128, 64), np.float32)
    for g, w in enumerate(POOL_W):
        cst[:, g * 16:(g + 1) * 16] = (1.0 / np.minimum(np.arange(1, 17), w)).astype(np.float32)[None, :]
    f = lambda a: np.ascontiguousarray(np.asarray(a, np.float32))

    def panels(w):
        w = np.asarray(w, np.float32)
        L, K_, M_ = w.shape
        k, m = K_ // 128, M_ // 128
        return np.ascontiguousarray(w.reshape(L, k, 128, m, 128).transpose(0, 3, 2, 1, 4)).reshape(L * m * 128, k * 128)

    def blocks(w):
        w = np.asarray(w, np.float32)
        J = w.shape[0]
        return np.ascontiguousarray(w.reshape(J, 16, 2, 128, 256).transpose(0, 1, 3, 2, 4)).reshape(J * 16 * 128, 512)
    return {
        "xT": f(np.asarray(x[b]).T),
        "pT": f(np.transpose(np.asarray(p[:, b]), (0, 2, 1)).reshape(DEPTH * PLE, S)),
        "w_in": panels(w_in),
        "w_out": panels(w_out),
        "w_gate": panels(w_ple_gate),
        "w_ple": panels(w_ple),
        "pool_w": panels(np.asarray(pool_w).reshape(8, 1024, 1024)),
        "lru_wa": blocks(lru_wa),
        "lru_wx": blocks(lru_wx),
        "pv": pv.reshape(DEPTH * 128, NPV),
        "cst": cst,
    }


N_LAUNCH_CORES = 2


def kernel(**inputs):
    nc = build_program()
    maps = [make_inputs(b, **inputs) for b in range(2)]
    in_maps = [maps[c % 2] for c in range(N_LAUNCH_CORES)]
    res = run_bass_kernel_spmd(nc, in_maps, core_ids=list(range(N_LAUNCH_CORES)))
    out = np.stack([np.ascontiguousarray(res.results[b]["yT"].T) for b in range(2)], axis=0)
    return out.astype(np.float32)
```

```python
import contextlib
import numpy as np
import concourse.bass as bass
import concourse.mybir as mybir
from concourse.bass_utils import run_bass_kernel_spmd

F32 = mybir.dt.float32
BF16 = mybir.dt.bfloat16
AF = mybir.ActivationFunctionType
ALU = mybir.AluOpType

D = 2048
E = 4096
S = 4096
T = 512
NT = S // T
KD = D // 128
KE = E // 128
DEPTH = 4
PLE = 256
EPS = 1e-6
NPV = 336
HALO = 16
UW = HALO + T
NU = 18
UNIT = 1024
NTMP = 8
NUCF = 4
NUB = 3
NPS = 6
POOL_W = (2, 4, 8, 16)

ENGS = ("pe", "act", "dve", "sp", "pool")


class Buf:
    __slots__ = ("lw", "rd")

    def __init__(self):
        self.lw = None
        self.rd = {}


class DSem:
    def __init__(self, key):
        self.key = key
        self.n = 0


class Sched:
    def __init__(self):
        self.streams = {e: [] for e in ENGS}
        self.cnt = {e: 0 for e in ENGS}
        self.seen = {e: {} for e in ENGS}
        self.final = []

    def _waits(self, eng, reads, writes):
        need = {}

        def add(tok, raw):
            key, val, peng, isdma = tok
            if (not isdma) and peng == eng and not raw:
                return
            if need.get(key, 0) < val:
                need[key] = val

        for b in reads:
            if b.lw is not None:
                add(b.lw, True)
        for b in writes:
            if b.lw is not None:
                add(b.lw, False)
            for tok in b.rd.values():
                add(tok, False)
        out = []
        seen = self.seen[eng]
        for key, val in need.items():
            if seen.get(key, 0) < val:
                seen[key] = val
                out.append((key, val))
        return out

    def _commit(self, tok, reads, writes):
        for b in reads:
            old = b.rd.get(tok[0])
            if old is None or old[1] < tok[1]:
                b.rd[tok[0]] = tok
        for b in writes:
            b.lw = tok
            b.rd = {}

    def op(self, eng, fn, reads=(), writes=()):
        waits = self._waits(eng, reads, writes)
        self.cnt[eng] += 1
        key = ("E", eng)
        tok = (key, self.cnt[eng], eng, False)
        self.streams[eng].append((waits, fn, (key, 1)))
        self._commit(tok, reads, writes)
        return tok

    def dma(self, eng, dsem, fn, reads=(), writes=()):
        waits = self._waits(eng, reads, writes)
        dsem.n += 16
        tok = (dsem.key, dsem.n, eng, True)
        self.streams[eng].append((waits, fn, (dsem.key, 16)))
        self._commit(tok, reads, writes)
        return tok

    def mm(self, out_buf, out_ap, pairs, reads):
        waits = self._waits("pe", reads, [out_buf])
        self.cnt["pe"] += 1
        key = ("E", "pe")
        tok = (key, self.cnt["pe"], "pe", False)
        n = len(pairs)
        for i, (l, r) in enumerate(pairs):
            def fn(pe, l=l, r=r, st=(i == 0), sp=(i == n - 1)):
                return pe.matmul(out_ap, l, r, start=st, stop=sp)
            self.streams["pe"].append((waits if i == 0 else [], fn, (key, 1) if i == n - 1 else None))
        self._commit(tok, reads, [out_buf])
        return tok

    def mm_seq(self, out_buf, out_ap, items, common):
        n = len(items)
        key = ("E", "pe")
        allr = list(common)
        for i, (l, r, rb) in enumerate(items):
            rds = list(rb) + (list(common) if i == 0 else [])
            waits = self._waits("pe", rds, [out_buf] if i == 0 else [])
            allr += list(rb)

            def fn(pe, l=l, r=r, st=(i == 0), sp=(i == n - 1)):
                return pe.matmul(out_ap, l, r, start=st, stop=sp)
            self.streams["pe"].append((waits, fn, (key, 1) if i == n - 1 else None))
        self.cnt["pe"] += 1
        tok = (key, self.cnt["pe"], "pe", False)
        self._commit(tok, allr, [out_buf])
        return tok

    def mm_seq2(self, groups, head_n):
        key = ("E", "pe")
        allr = [list(g[3]) for g in groups]

        def emit(gi, i, first, last):
            out_buf, out_ap, items, common = groups[gi]
            l, r, rb = items[i]
            n = len(items)
            rds = list(rb) + (list(common) if first else [])
            waits = self._waits("pe", rds, [out_buf] if first else [])
            allr[gi] += list(rb)

            def fn(pe, l=l, r=r, st=(i == 0), sp=(i == n - 1), oa=out_ap):
                return pe.matmul(oa, l, r, start=st, stop=sp)
            self.streams["pe"].append((waits, fn, (key, 1) if last else None))
            if last:
                self.cnt["pe"] += 1
                tok = (key, self.cnt["pe"], "pe", False)
                self._commit(tok, allr[gi], [out_buf])

        for gi in range(len(groups)):
            for i in range(head_n):
                emit(gi, i, i == 0, False)
        for gi in range(len(groups)):
            n = len(groups[gi][2])
            for i in range(head_n, n):
                emit(gi, i, False, i == n - 1)

    def mm1(self, out_buf, out_ap, l, r, start, stop, reads):
        writes = [out_buf] if start else []
        rds = list(reads)
        waits = self._waits("pe", rds, writes)
        self.cnt["pe"] += 1
        key = ("E", "pe")
        tok = (key, self.cnt["pe"], "pe", False)

        def fn(pe):
            return pe.matmul(out_ap, l, r, start=start, stop=stop)
        self.streams["pe"].append((waits, fn, (key, 1)))
        if start:
            self._commit(tok, rds, [out_buf])
        else:
            self._commit(tok, rds, [])
            out_buf.lw = tok
        return tok

    def emit(self, block, sems):
        def make(name):
            def body(eng):
                for waits, fn, inc in self.streams[name]:
                    for key, val in waits:
                        eng.wait_ge(sems[key], val)
                    ins = fn(eng)
                    if inc is not None:
                        ins.then_inc(sems[inc[0]], inc[1])
                if name == "sp":
                    for key, val in self.final:
                        eng.wait_ge(sems[key], val)
            return body
        block.tensor(make("pe"))
        block.scalar(make("act"))
        block.vector(make("dve"))
        block.sync(make("sp"))
        block.gpsimd(make("pool"))


class Rot:
    def __init__(self, aps):
        self.items = [(Buf(), ap) for ap in aps]
        self.i = 0

    def next(self):
        it = self.items[self.i % len(self.items)]
        self.i += 1
        return it


def build_program(n_layers=DEPTH, n_tiles=NT):
    nc = bass.Bass("TRN2", target_bir_lowering=False)
    sc = Sched()

    def din(name, shape):
        return nc.dram_tensor(name, shape, F32, kind="ExternalInput").ap()

    xT = din("xT", [D, S])
    pT = din("pT", [DEPTH * PLE, S])
    w_in = din("w_in", [DEPTH * 64 * 128, D])
    w_out = din("w_out", [DEPTH * 16 * 128, E])
    w_gate = din("w_gate", [DEPTH * 16 * 128, D])
    w_ple = din("w_ple", [DEPTH * 16 * 128, PLE])
    pool_w = din("pool_w", [2 * 4 * 8 * 128, 1024])
    lru_wa = din("lru_wa", [2 * 16 * 128, 512])
    lru_wx = din("lru_wx", [2 * 16 * 128, 512])
    pv_d = din("pv", [DEPTH * 128, NPV])
    cst_d = din("cst", [128, 64])
    yT = nc.dram_tensor("yT", [D, S], F32, kind="ExternalOutput").ap()
    xs = [nc.dram_tensor("xs0", [D, S], F32).ap(), nc.dram_tensor("xs1", [D, S], F32).ap()]
    hs = [nc.dram_tensor("hs0", [D, S], BF16).ap(), nc.dram_tensor("hs1", [D, S], BF16).ap()]

    es = contextlib.ExitStack()
    with es:
        def sb(name, shape, dt):
            return es.enter_context(nc.sbuf_tensor(name, shape, dt))

        xt = sb("xt", [128, KD, T], F32)
        hb = sb("hb", [128, KD, T], BF16)
        yb = sb("yb", [128, KE, T], BF16)
        ob = sb("ob", [128, KD, T], F32)
        obh = ob[:, :, :].bitcast(BF16)
        wsl = sb("wsl", [128, NU * UNIT], BF16)
        ubuf = sb("ubuf", [128, NUB, UW], F32)
        mixf = sb("mixf", [128, 4096 + 2 * UW], F32)
        dbuf = mixf[:, 0:4096].bitcast(BF16).rearrange("p (a b t) -> p a b t", a=2, b=8)
        wtmp = mixf[:, 4096:4096 + 2 * UW].rearrange("p (a t) -> p a t", a=2)
        ftmp = sb("ftmp", [128, NTMP, T], F32)
        hst = mixf[:, 0:4096].bitcast(BF16).rearrange("p (k t) -> p k t", k=KD)
        ucft = mixf[:, 0:NUCF * T].rearrange("p (a t) -> p a t", a=NUCF)
        sqb = sb("sqb", [128, 4, T], BF16)
        ucb = mixf[:, NUCF * T:NUCF * T + 2 * T].bitcast(BF16).rearrange("p (a t) -> p a t", a=4)
        rstd = sb("rstd", [128, T], F32)
        rstdb = sb("rstdb", [128, T], F32)
        pb = sb("pb", [128, 2, T], BF16)
        pvt = sb("pvt", [128, 2, NPV], F32)
        dvt = sb("dvt", [128, 2, 128], F32)
        carry = sb("carry", [128, KE, HALO], F32)
        state = sb("state", [128, KE], F32)
        ones = sb("ones", [128, 128], BF16)
        cst = sb("cst_sb", [128, 64], F32)
        fence = sb("fence", [128, 8], F32)
        psums = [es.enter_context(nc.psum_tensor(f"ps{i}", [128, T], F32)) for i in range(8)]

        B_x = [Buf() for _ in range(KD)]
        B_h = [Buf() for _ in range(KD)]
        B_y = [Buf() for _ in range(KE)]
        B_o = [Buf() for _ in range(KD)]
        B_w = [Buf() for _ in range(NU)]
        B_d = [[Buf() for _ in range(8)] for _ in range(2)]
        B_rstd = Buf()
        B_rstdb = Buf()
        B_hst = [Buf() for _ in range(KD)]
        B_pb = Buf()
        B_pv = [Buf(), Buf()]
        B_dv = [Buf(), Buf()]
        B_carry = [Buf() for _ in range(KE)]
        B_state = [Buf() for _ in range(KE)]
        B_ones = Buf()
        B_cst = Buf()
        B_fence = Buf()
        B_xs = [[Buf() for _ in range(n_tiles)] for _ in range(2)]
        B_hs = [[Buf() for _ in range(n_tiles)] for _ in range(2)]
        B_out = [Buf() for _ in range(n_tiles)]
        B_in = Buf()
        R_ps = Rot([psums[i][:, :] for i in range(NPS)])
        R_ss = Rot([psums[NPS + i][:, :] for i in range(2)])
        R_u = Rot([ubuf[:, i, :] for i in range(NUB)])
        R_wt = Rot([wtmp[:, i, :] for i in range(2)])
        R_t = Rot([ftmp[:, i, :] for i in range(NTMP)])
        R_ucf = Rot([ucft[:, i, :] for i in range(NUCF)])
        R_sq = Rot([sqb[:, i, :] for i in range(4)])
        R_ucb = Rot([ucb[:, i, :] for i in range(4)])

        wsem = [DSem(("D", f"w{i}")) for i in range(NU)]
        xsem = [DSem(("D", f"x{i}")) for i in range(4)]
        osem = DSem(("D", "o"))
        hssem = DSem(("D", "hs"))
        hlsem = [DSem(("D", f"hl{i}")) for i in range(4)]
        deferred = []
        psem = DSem(("D", "p"))
        vsem = DSem(("D", "v"))
        wctr = [0]

        def wload(src_ap, view):
            n = 1
            for dsz in src_ap.shape[1:]:
                n *= dsz
            nun = (n + UNIT - 1) // UNIT
            if wctr[0] + nun > NU:
                wctr[0] = 0
            s = wctr[0]
            wctr[0] += nun
            dst = wsl[:, s * UNIT:s * UNIT + n]
            if view is not None:
                dst = dst.rearrange(view[0], **view[1])
            bufs = B_w[s:s + nun]
            sc.dma("pool", wsem[s], lambda g, d=dst, a=src_ap: g.dma_start(out=d, in_=a),
                   reads=[B_in], writes=bufs)
            return bufs, dst

        def act(out, in_, func, reads, writes, bias=0.0, scale=1.0):
            sc.op("act", lambda a: a.activation(out=out, in_=in_, func=func, bias=bias, scale=scale),
                  reads=reads, writes=writes)

        def tanh_half(tb, tap, src_ap, src_bufs, hbias=None):
            if hbias is None:
                act(tap, src_ap, AF.Tanh, src_bufs, [tb], scale=0.5)
            else:
                act(tap, src_ap, AF.Tanh, src_bufs + [hbias[0]], [tb], bias=hbias[1], scale=0.5)

        class Stats:
            def __init__(self, n, lag):
                self.n, self.lag, self.i, self.pend = n, lag, 0, []
                self.ssb, self.ssap = R_ss.next()

            def add(self, src_ap, src_bufs):
                qb, qap = R_sq.next()
                act(qap, src_ap, AF.Square, src_bufs, [qb])
                self.pend.append((qb, qap))
                while len(self.pend) > self.lag:
                    self._one()

            def _one(self):
                qb, qap = self.pend.pop(0)
                sc.mm1(self.ssb, self.ssap, ones[:, :], qap, self.i == 0, self.i == self.n - 1, [qb, B_ones])
                self.i += 1

            def finish(self, mean_scale, lnbias=0.0, alt=False):
                while self.pend:
                    self._one()
                finish_rstd(self.ssb, self.ssap, mean_scale, lnbias, alt)

        def finish_rstd(ssb, ssap, mean_scale, lnbias=0.0, alt=False):
            tb, tap = R_t.next()
            act(tap, ssap, AF.Ln, [ssb], [tb], bias=EPS, scale=mean_scale)
            if alt:
                act(rstdb[:, :], tap, AF.Exp, [tb], [B_rstdb], bias=lnbias, scale=-0.5)
            else:
                act(rstd[:, :], tap, AF.Exp, [tb], [B_rstd], bias=lnbias, scale=-0.5)

        sc.op("dve", lambda v: v.memset(ones[:, :], 1.0), writes=[B_ones])
        sc.dma("sp", vsem, lambda q: q.dma_start(out=cst[:, :], in_=cst_d[:, :]), reads=[B_in], writes=[B_cst])

        pends = {}
        head_done = set()
        h_done = set()
        hs_stored = set()
        setup_done = set()

        def layer_ctx(li):
            par = li % 2
            return (li % 2 == 0, li // 2, pvt[:, par, :], dvt[:, par, :], B_pv[par], B_dv[par],
                    xT if li == 0 else xs[(li - 1) % 2], None if li == 0 else B_xs[(li - 1) % 2])

        def layer_setup(li):
            if li in setup_done:
                return
            setup_done.add(li)
            is_pool, jj, pvl, dvl, Bpv, Bdv, src, src_b = layer_ctx(li)
            sc.dma("sp", vsem, lambda q, d=pvl, a=pv_d[li * 128:(li + 1) * 128, :]: q.dma_start(out=d, in_=a),
                   reads=[B_in], writes=[Bpv])
            if is_pool:
                sc.op("dve", lambda v, d=dvl, p=pvl: v.tensor_tensor(out=d[:, 0:32], in0=p[:, 64:96], in1=p[:, 96:128], op=ALU.mult),
                      reads=[Bpv], writes=[Bdv])
            else:
                sc.op("dve", lambda v, d=dvl, p=pvl: v.tensor_scalar(out=d[:, 0:64], in0=p[:, 96:160], scalar1=0.5, scalar2=None, op0=ALU.mult),
                      reads=[Bpv], writes=[Bdv])
                act(dvl[:, 64:96], pvl[:, 160:192], AF.Exp, [Bpv], [Bdv], scale=-1.0)
                act(dvl[:, 64:96], dvl[:, 64:96], AF.Ln, [Bdv], [Bdv], bias=1.0)
                sc.op("dve", lambda v, d=dvl: v.tensor_scalar(out=d[:, 96:128], in0=d[:, 64:96], scalar1=-4.0, scalar2=None, op0=ALU.mult),
                      reads=[Bdv], writes=[Bdv])
                sc.op("dve", lambda v, d=dvl: v.tensor_scalar(out=d[:, 64:96], in0=d[:, 64:96], scalar1=-8.0, scalar2=None, op0=ALU.mult),
                      reads=[Bdv], writes=[Bdv])
            sc.op("dve", lambda v: v.memset(carry[:, :, :], 0.0), writes=B_carry)
            sc.op("dve", lambda v: v.memset(state[:, :], 0.0), writes=B_state)


        def make_load_x(li, tj):
            is_pool, jj, pvl, dvl, Bpv, Bdv, src, src_b = layer_ctx(li)
            c0 = tj * T

            def load_x(tj=tj, c0=c0):
                for q4 in range(4):
                    ks = slice(q4 * 4, q4 * 4 + 4)
                    sap = src[q4 * 512:(q4 + 1) * 512, c0:c0 + T].rearrange("(k p) t -> p k t", p=128)
                    rd = [B_in] if src_b is None else [src_b[tj]]
                    sc.dma("sp", xsem[q4], lambda q, d=xt[:, ks, :], a=sap: q.dma_start(out=d, in_=a),
                           reads=rd, writes=B_x[q4 * 4:q4 * 4 + 4])

            return load_x

        def phase_a(li, tj, part):
            is_pool, jj, pvl, dvl, Bpv, Bdv, src, src_b = layer_ctx(li)
            c0 = tj * T
            use_hload = (li >= 1)
            load_x = make_load_x(li, tj)
            if part in ('head', 'all'):
                sc.op("dve", lambda v: v.memset(fence[:, 0:1], 0.0), writes=B_hst + [B_fence])
            if part in ('h', 'all'):
                if not use_hload:
                    load_x()
                    st = Stats(KD, 2)
                    for k in range(KD):
                        st.add(xt[:, k, :], [B_x[k]])
                    st.finish(1.0 / D)
                    for k in range(KD):
                        sc.op("dve", lambda v, k=k, p=pvl: v.scalar_tensor_tensor(
                            out=hb[:, k, :], in0=xt[:, k, :], scalar=p[:, k:k + 1], in1=rstd[:, :],
                            op0=ALU.mult, op1=ALU.mult), reads=[B_x[k], Bpv, B_rstd], writes=[B_h[k]])
                else:
                    for q4 in range(4):
                        ks = slice(q4 * 4, q4 * 4 + 4)
                        sap = hs[(li - 1) % 2][q4 * 512:(q4 + 1) * 512, c0:c0 + T].rearrange("(k p) t -> p k t", p=128)
                        sc.dma("sp", hlsem[q4], lambda q, d=hb[:, ks, :], a=sap: q.dma_start(out=d, in_=a),
                               reads=[B_hs[(li - 1) % 2][tj]], writes=B_h[q4 * 4:q4 * 4 + 4])

            wrow = li * D

            def in_chunk(c):
                a = w_in[(li * 64 + c) * 128:(li * 64 + c + 1) * 128, :]
                return wload(a, ("p (k n) -> p k n", dict(k=KD)))

            def proj_in(c):
                wb, wap = in_chunk(c)
                pbuf, pap_ = R_ps.next()
                sc.mm(pbuf, pap_, [(wap[:, k, :], hb[:, k, :]) for k in range(KD)], reads=wb + B_h)
                return pbuf, pap_

            def z_finish(m, zb, zap, tb, tap, hsb, hsap):
                sc.op("dve", lambda v, t=tap, h=hsap, m=m: v.tensor_tensor(
                    out=yb[:, m, :], in0=h, in1=t, op=ALU.mult), reads=[tb, hsb], writes=[B_y[m]])

            def z_to_y(m, zb, zap, hsb, hsap):
                tb, tap = R_t.next()
                act(tap, zap, AF.Silu, [zb], [tb])
                z_finish(m, zb, zap, tb, tap, hsb, hsap)

            if is_pool:
                def u_group(g):
                    w = POOL_W[g]
                    for half_pair in range(4):
                        for half in range(2):
                            m = g * 8 + half_pair * 2 + half
                            pbuf, pap_ = proj_in(m)
                            ub, uap = R_u.next()
                            act(uap[:, HALO:UW], pap_, AF.Copy, [pbuf], [ub])
                            sc.op("dve", lambda v, u=uap, m=m: v.tensor_copy(out=u[:, 0:HALO], in_=carry[:, m, :]),
                                  reads=[B_carry[m]], writes=[ub])
                            cur_b, cur = ub, uap
                            lo = -HALO
                            step = 1
                            while step < w:
                                nb, nap = R_wt.next()
                                nlo = lo + step
                                a0 = HALO + nlo
                                sc.op("dve", lambda v, o=nap, c=cur, a0=a0, s=step: v.tensor_tensor(
                                    out=o[:, a0:UW], in0=c[:, a0:UW], in1=c[:, a0 - s:UW - s], op=ALU.add),
                                    reads=[cur_b], writes=[nb])
                                cur_b, cur, lo = nb, nap, nlo
                                step *= 2
                            db = B_d[g % 2][m % 8]
                            dap = dbuf[:, g % 2, m % 8, :]
                            sc.op("dve", lambda v, o=dap, c=cur, u=uap, w=w: v.scalar_tensor_tensor(
                                out=o, in0=c[:, HALO:UW], scalar=1.0 / w, in1=u[:, HALO:UW],
                                op0=ALU.mult, op1=ALU.subtract), reads=[cur_b, ub], writes=[db])
                            if tj == 0:
                                tb, tap = R_t.next()
                                sc.op("dve", lambda v, t=tap, c=cur, g=g: v.tensor_tensor(
                                    out=t[:, 0:16], in0=c[:, HALO:HALO + 16], in1=cst[:, g * 16:(g + 1) * 16], op=ALU.mult),
                                    reads=[cur_b, B_cst], writes=[tb])
                                sc.op("dve", lambda v, o=dap, t=tap, u=uap: v.tensor_tensor(
                                    out=o[:, 0:16], in0=t[:, 0:16], in1=u[:, HALO:HALO + 16], op=ALU.subtract),
                                    reads=[tb, ub], writes=[db])
                            sc.op("dve", lambda v, u=uap, m=m: v.tensor_copy(out=carry[:, m, :], in_=u[:, T:UW]),
                                  reads=[ub], writes=[B_carry[m]])

                def pz_group(g):
                    prow = (jj * 4 + g) * 1024
                    for hf in range(1):
                        for pr in range(1):
                            for mo in range(8):
                                m = g * 8 + mo
                                a = pool_w[((jj * 4 + g) * 8 + mo) * 128:((jj * 4 + g) * 8 + mo + 1) * 128, :]
                                pwb, pwap = wload(a, ("p (k n) -> p k n", dict(k=8)))
                                zb, zap = proj_in(KE + m)
                                ybf, yap = R_ps.next()
                                sc.mm(ybf, yap, [(pwap[:, ki, :], dbuf[:, g % 2, ki, :]) for ki in range(8)],
                                      reads=pwb + B_d[g % 2])
                                hsb, hsap = R_t.next()
                                sc.op("act", lambda a_, o=hsap, i=yap, m=m, p=pvl, d=dvl: a_.activation(
                                    out=o, in_=i, func=AF.Identity, bias=d[:, m:m + 1], scale=p[:, 96 + m:97 + m]),
                                    reads=[ybf, Bpv, Bdv], writes=[hsb])
                                z_to_y(m, zb, zap, hsb, hsap)

                if part in ('head', 'all'):
                    u_group(0)
                if part in ('rest', 'all'):
                    for g in range(1, 4):
                        u_group(g)
                        pz_group(g - 1)
                    pz_group(3)
            else:
                pend = pends.setdefault((li, tj), {})

                def u_block(q):
                    res = []
                    for half in range(2):
                        m = 2 * q + half
                        pbuf, pap_ = proj_in(m)
                        ub, uap = R_u.next()
                        act(uap[:, HALO:UW], pap_, AF.Copy, [pbuf], [ub])
                        sc.op("dve", lambda v, u=uap, m=m: v.tensor_copy(out=u[:, HALO - 3:HALO], in_=carry[:, m, HALO - 3:HALO]),
                              reads=[B_carry[m]], writes=[ub])
                        ucf_b, ucf = R_ucf.next()
                        cw = 192
                        sc.op("dve", lambda v, o=ucf, u=uap, m=m, p=pvl: v.tensor_scalar(
                            out=o, in0=u[:, HALO - 3:UW - 3], scalar1=p[:, 192 + m:193 + m], scalar2=p[:, 64 + m:65 + m],
                            op0=ALU.mult, op1=ALU.add), reads=[ub, Bpv], writes=[ucf_b])
                        for kk in range(1, 4):
                            sc.op("dve", lambda v, o=ucf, u=uap, m=m, kk=kk, p=pvl: v.scalar_tensor_tensor(
                                out=o, in0=u[:, HALO - 3 + kk:UW - 3 + kk], scalar=p[:, cw + kk * 32 + m:cw + kk * 32 + m + 1],
                                in1=o, op0=ALU.mult, op1=ALU.add), reads=[ub, Bpv, ucf_b], writes=[ucf_b])
                        cb_, cap = R_ucb.next()
                        sc.op("dve", lambda v, o=cap, i=ucf: v.tensor_copy(out=o, in_=i), reads=[ucf_b], writes=[cb_])
                        sc.op("dve", lambda v, u=uap, m=m: v.tensor_copy(out=carry[:, m, HALO - 3:HALO], in_=u[:, UW - 3:UW]),
                              reads=[ub], writes=[B_carry[m]])
                        res.append((m, ucf_b, ucf, cb_, cap))
                    pend[q] = res

                gate_w = {}

                def g_block(q):
                    r0 = (jj * 16 + q) * 128
                    aa = lru_wa[r0:r0 + 128, :]
                    ax = lru_wx[r0:r0 + 128, :]
                    wab, waap = wload(aa, ("p (k n) -> p k n", dict(k=2)))
                    wxb, wxap = wload(ax, ("p (k n) -> p k n", dict(k=2)))
                    res = pend.pop(q)
                    cs = []
                    for jo in range(2):
                        m = res[jo][0]
                        rb, rap = R_ps.next()
                        sc.mm(rb, rap, [(waap[:, ii, jo * 128:(jo + 1) * 128], res[ii][4]) for ii in range(2)],
                              reads=wab + [res[0][3], res[1][3]])
                        ib, iap = R_ps.next()
                        sc.mm(ib, iap, [(wxap[:, ii, jo * 128:(jo + 1) * 128], res[ii][4]) for ii in range(2)],
                              reads=wxb + [res[0][3], res[1][3]])
                        zb, zap = proj_in(KE + m)
                        cs.append((rb, rap, ib, iap, zb, zap))
                    tm = []
                    for jo in range(2):
                        m = res[jo][0]
                        rb, rap, ib, iap, zb, zap = cs[jo]
                        t1b, t1 = R_t.next()
                        tab, ta = R_t.next()
                        t2b, t2 = R_t.next()
                        t3b, t3 = R_t.next()
                        tanh_half(t1b, t1, rap, [rb], hbias=(Bdv, dvl[:, m:m + 1]))
                        tanh_half(t2b, t2, iap, [ib], hbias=(Bdv, dvl[:, 32 + m:33 + m]))
                        act(t3, zap, AF.Silu, [zb], [t3b])
                        tm.append((t1b, t1, tab, ta, t2b, t2, t3b, t3))
                    for jo in range(2):
                        m = res[jo][0]
                        t1b, t1, tab, ta, t2b, t2, t3b, t3 = tm[jo]
                        sc.op("act", lambda a_, o=ta, i=t1, m=m, d=dvl: a_.activation(
                            out=o, in_=i, func=AF.Exp, bias=d[:, 96 + m:97 + m], scale=d[:, 96 + m:97 + m]),
                            reads=[t1b, Bdv], writes=[tab])
                        sc.op("act", lambda a_, o=t1, i=t1, m=m, d=dvl: a_.activation(
                            out=o, in_=i, func=AF.Exp, bias=d[:, 64 + m:65 + m], scale=d[:, 64 + m:65 + m]),
                            reads=[t1b, Bdv], writes=[t1b])
                        act(t1, t1, AF.Ln, [t1b], [t1b], bias=1.0, scale=-1.0)
                        act(t1, t1, AF.Exp, [t1b], [t1b], scale=0.5)
                    for jo in range(2):
                        m, ucf_b, ucf, _, _ = res[jo]
                        rb, rap, ib, iap, zb, zap = cs[jo]
                        t1b, t1, tab, ta, t2b, t2, t3b, t3 = tm[jo]
                        sc.op("dve", lambda v, a=t2, b=ucf: v.scalar_tensor_tensor(
                            out=a, in0=a, scalar=1.0, in1=b, op0=ALU.add, op1=ALU.mult),
                            reads=[t2b, ucf_b], writes=[t2b])
                        sc.op("dve", lambda v, a=t2, b=t1: v.scalar_tensor_tensor(
                            out=a, in0=a, scalar=0.5, in1=b, op0=ALU.mult, op1=ALU.mult),
                            reads=[t1b, t2b], writes=[t2b])
                        sc.op("dve", lambda v, o=t1, a=ta, b=t2, m=m: v.tensor_tensor_scan(
                            out=o, data0=a, data1=b, initial=state[:, m:m + 1], op0=ALU.mult, op1=ALU.add),
                            reads=[tab, t2b, B_state[m]], writes=[t1b])
                        sc.op("dve", lambda v, o=t1, m=m: v.tensor_copy(out=state[:, m:m + 1], in_=o[:, T - 1:T]),
                              reads=[t1b], writes=[B_state[m]])
                        z_finish(m, zb, zap, t3b, t3, t1b, t1)

                if part in ('head', 'all'):
                    u_block(0)
                    u_block(1)
                if part in ('rest', 'all'):
                    g_block(0)
                    for q in range(2, 16):
                        u_block(q)
                        g_block(q - 1)
                    g_block(15)


        for li in range(n_layers):
            is_pool = (li % 2 == 0)
            jj = li // 2
            par = li % 2
            src = xT if li == 0 else xs[(li - 1) % 2]
            src_b = None if li == 0 else B_xs[(li - 1) % 2]
            last = (li == n_layers - 1)
            dst = yT if last else xs[li % 2]
            dst_b = B_out if last else B_xs[li % 2]
            pvl = pvt[:, par, :]
            dvl = dvt[:, par, :]
            Bpv = B_pv[par]
            Bdv = B_dv[par]

            layer_setup(li)

            for tj in range(n_tiles):
                c0 = tj * T
                use_hload = (li >= 1)
                load_x = make_load_x(li, tj)
                if (li, tj) not in h_done:
                    phase_a(li, tj, 'h')
                if (li, tj) not in head_done:
                    phase_a(li, tj, 'head')
                phase_a(li, tj, 'rest')
                if tj + 1 < n_tiles:
                    nx = (li, tj + 1)
                elif li + 1 < n_layers:
                    nx = (li + 1, 0)
                else:
                    nx = None

                pap = pT[li * PLE:(li + 1) * PLE, c0:c0 + T].rearrange("(k p) t -> p k t", p=128)
                sc.dma("pool", psem, lambda g, a=pap: g.dma_start(out=pb[:, :, :], in_=a), reads=[B_in], writes=[B_pb])

                def next_h():
                    if nx is not None and nx[0] >= 1 and nx not in h_done:
                        layer_setup(nx[0])
                        phase_a(nx[0], nx[1], 'h')
                        h_done.add(nx)

                prev = deferred.pop(0) if deferred else None
                st2 = None
                if prev is not None:
                    prev[0]()
                    if prev[2]:
                        st2 = Stats(KD, 1)
                st = Stats(KD, 1 if st2 is not None else 2)
                if st2 is None or (nx is not None and (nx[0] - 1, nx[1]) in hs_stored):
                    next_h()
                orow = li * E
                x_loaded = not use_hload
                ws = []
                for mo in range(2):
                    a = w_out[(li * 16 + mo) * 128:(li * 16 + mo + 1) * 128, :]
                    wb, wap = wload(a, ("p (k n) -> p k n", dict(k=KE)))
                    pbuf, pap_ = R_ps.next()
                    ws.append((wb, wap, pbuf, pap_))
                sc.mm_seq2([(pbuf, pap_, [(wap[:, k, :], yb[:, k, :], [B_y[k]]) for k in range(KE)], wb)
                            for (wb, wap, pbuf, pap_) in ws], KE - 4)
                for mo in range(KD):
                    if mo < 2:
                        wb, wap, pbuf, pap_ = ws[mo]
                    else:
                        a = w_out[(li * 16 + mo) * 128:(li * 16 + mo + 1) * 128, :]
                        wb, wap = wload(a, ("p (k n) -> p k n", dict(k=KE)))
                        pbuf, pap_ = R_ps.next()
                        sc.mm(pbuf, pap_, [(wap[:, k, :], yb[:, k, :]) for k in range(KE)], reads=wb + B_y)
                    act(ob[:, mo, :], pap_, AF.Copy, [pbuf], [B_o[mo]])
                    st.add(pap_, [pbuf])
                    if st2 is not None:
                        if 2 <= mo < 10:
                            st2.add(xt[:, 2 * (mo - 2), :], [B_x[2 * (mo - 2)]])
                            st2.add(xt[:, 2 * (mo - 2) + 1, :], [B_x[2 * (mo - 2) + 1]])
                        elif mo == 10:
                            st2.finish(1.0 / D, alt=True)
                            prev[1]()
                            if not x_loaded:
                                load_x()
                                x_loaded = True
                            prev[3]()
                            next_h()
                    if not x_loaded and (prev is None or (st2 is None and mo == 3)):
                        load_x()
                        x_loaded = True
                st.finish(1.0 / D)
                for k in range(KD):
                    sc.op("dve", lambda v, k=k, p=pvl: v.scalar_tensor_tensor(
                        out=ob[:, k, :], in0=ob[:, k, :], scalar=p[:, 16 + k:17 + k], in1=rstd[:, :],
                        op0=ALU.mult, op1=ALU.mult), reads=[B_o[k], Bpv, B_rstd], writes=[B_o[k]])
                    sc.op("dve", lambda v, k=k: v.tensor_tensor(out=xt[:, k, :], in0=xt[:, k, :], in1=ob[:, k, :], op=ALU.add),
                          reads=[B_o[k], B_x[k]], writes=[B_x[k]])

                for k in range(KD):
                    sc.op("act", lambda a_, k=k, p=pvl: a_.activation(
                        out=yb[:, k, :], in_=xt[:, k, :], func=AF.Identity, scale=p[:, 32 + k:33 + k]),
                        reads=[B_x[k], Bpv], writes=[B_y[k]])
                if nx is not None and nx[0] >= 1:
                    phase_a(nx[0], nx[1], 'head')
                    head_done.add(nx)
                st = Stats(KD, 2)
                for k in range(KD):
                    st.add(xt[:, k, :], [B_x[k]])
                st.finish(1.0 / D)
                st = Stats(KD, 2)
                grow = li * D
                for mo in range(KD):
                    a = w_gate[(li * 16 + mo) * 128:(li * 16 + mo + 1) * 128, :]
                    wb, wap = wload(a, ("p (k n) -> p k n", dict(k=KD)))
                    a2 = w_ple[(li * 16 + mo) * 128:(li * 16 + mo + 1) * 128, :]
                    pwb, pwap = wload(a2, ("p (k n) -> p k n", dict(k=2)))
                    gb, gap = R_ps.next()
                    if mo == 0:
                        sc.mm_seq(gb, gap, [(wap[:, k, :], yb[:, k, :], [B_y[k]]) for k in range(KD)], wb)
                    else:
                        sc.mm(gb, gap, [(wap[:, k, :], yb[:, k, :]) for k in range(KD)], reads=wb + B_y[0:KD])
                    eb, eap = R_ps.next()
                    sc.mm(eb, eap, [(pwap[:, k2, :], pb[:, k2, :]) for k2 in range(2)], reads=pwb + [B_pb])
                    tb, tap = R_t.next()
                    sc.op("dve", lambda v, t=tap, g_=gap: v.tensor_tensor(out=t, in0=g_, in1=rstd[:, :], op=ALU.mult),
                          reads=[gb, B_rstd], writes=[tb])
                    tanh_half(tb, tap, tap, [tb])
                    sc.op("dve", lambda v, mo=mo, e=eap, t=tap: v.scalar_tensor_tensor(
                        out=ob[:, mo, :], in0=t, scalar=1.0, in1=e, op0=ALU.add, op1=ALU.mult),
                        reads=[eb, tb], writes=[B_o[mo]])
                    st.add(ob[:, mo, :], [B_o[mo]])
                nxt = None
                if tj + 1 < n_tiles:
                    nxt = li
                elif li + 1 < n_layers:
                    nxt = li + 1

                def tail1(st=st, pvl=pvl, Bpv=Bpv, dst=dst, dst_b=dst_b, tj=tj, c0=c0, last=last):
                    st.finish(0.25 / D, float(np.log(0.5)))
                    for k in range(KD):
                        sc.op("dve", lambda v, k=k, p=pvl: v.scalar_tensor_tensor(
                            out=ob[:, k, :], in0=ob[:, k, :], scalar=p[:, 48 + k:49 + k], in1=rstd[:, :],
                            op0=ALU.mult, op1=ALU.mult), reads=[B_o[k], Bpv, B_rstd], writes=[B_o[k]])
                        sc.op("dve", lambda v, k=k: v.tensor_tensor(out=xt[:, k, :], in0=xt[:, k, :], in1=ob[:, k, :], op=ALU.add),
                              reads=[B_o[k], B_x[k]], writes=[B_x[k]])
                    for q4 in range(4):
                        ks = slice(q4 * 4, q4 * 4 + 4)
                        dap = dst[q4 * 512:(q4 + 1) * 512, c0:c0 + T].rearrange("(k p) t -> p k t", p=128)
                        tok = sc.dma("sp", osem, lambda q, s_=xt[:, ks, :], a=dap: q.dma_start(out=a, in_=s_),
                                     reads=B_x[q4 * 4:q4 * 4 + 4], writes=[dst_b[tj]])
                        if last:
                            sc.final.append((tok[0], tok[1]))

                def tail2(pvl=pvl, Bpv=Bpv, li=li, tj=tj, c0=c0):
                    alias = [b for b, _ in R_ucb.items] + [b for b, _ in R_ucf.items] + [b for b, _ in R_wt.items] \
                        + B_d[0] + B_d[1]
                    sc.op("dve", lambda v: v.memset(fence[:, 0:1], 0.0), writes=alias + [B_fence])
                    for k in range(KD):
                        sc.op("dve", lambda v, k=k, p=pvl: v.scalar_tensor_tensor(
                            out=hst[:, k, :], in0=xt[:, k, :], scalar=p[:, 320 + k:321 + k], in1=rstdb[:, :],
                            op0=ALU.mult, op1=ALU.mult), reads=[B_x[k], Bpv, B_rstdb], writes=[B_hst[k]])

                def tail2b(li=li, tj=tj, c0=c0):
                    hs_stored.add((li, tj))
                    for q4 in range(4):
                        ks = slice(q4 * 4, q4 * 4 + 4)
                        dap = hs[li % 2][q4 * 512:(q4 + 1) * 512, c0:c0 + T].rearrange("(k p) t -> p k t", p=128)
                        sc.dma("sp", hssem, lambda q, s_=hst[:, ks, :], a=dap: q.dma_start(out=a, in_=s_),
                               reads=B_hst[q4 * 4:q4 * 4 + 4], writes=[B_hs[li % 2][tj]])

                if nxt is not None and nxt >= 1:
                    deferred.append((tail1, tail2, not last, tail2b))
                else:
                    tail1()
                    if not last:
                        st2 = Stats(KD, 2)
                        for k in range(KD):
                            st2.add(xt[:, k, :], [B_x[k]])
                        st2.finish(1.0 / D, alt=True)
                        tail2()
                        tail2b()

        sems = {}
        for e in ENGS:
            sems[("E", e)] = es.enter_context(nc.semaphore(f"e_{e}"))
        for ds in wsem + xsem + hlsem + [osem, hssem, psem, vsem]:
            sems[ds.key] = es.enter_context(nc.semaphore("d_" + ds.key[1]))
        block = es.enter_context(nc.Block())
        sc.emit(block, sems)
    return nc


def _vec_pm(v):
    return np.ascontiguousarray(np.asarray(v, np.float32).reshape(-1, 128).T)


def make_inputs(b, x, p, w_in, w_out, g_pre, g_post, pool_w, pool_b, pool_scale,
                conv_w, conv_b, lru_wa, lru_ba, lru_wx, lru_bx, lru_L,
                w_ple, w_ple_gate, g_ple_in, g_ple_out):
    pv = np.zeros((DEPTH, 128, NPV), np.float32)
    for i in range(DEPTH):
        j = i // 2
        pv[i, :, 0:16] = _vec_pm(g_pre[i])
        pv[i, :, 16:32] = _vec_pm(g_post[i])
        pv[i, :, 32:48] = _vec_pm(g_ple_in[i])
        pv[i, :, 48:64] = _vec_pm(g_ple_out[i])
        if i + 1 < DEPTH:
            pv[i, :, 320:336] = _vec_pm(g_pre[i + 1])
        if i % 2 == 0:
            pv[i, :, 64:96] = _vec_pm(pool_b[j])
            pv[i, :, 96:128] = _vec_pm(pool_scale[j])
        else:
            pv[i, :, 64:96] = _vec_pm(conv_b[j])
            pv[i, :, 96:128] = _vec_pm(lru_ba[j])
            pv[i, :, 128:160] = _vec_pm(lru_bx[j])
            pv[i, :, 160:192] = _vec_pm(lru_L[j])
            for k in range(4):
                pv[i, :, 192 + k * 32:192 + (k + 1) * 32] = _vec_pm(conv_w[j, k])
    cst = np.zeros((128, 64), np.float32)
    for g, w in enumerate(POOL_W):
        cst[:, g * 16:(g + 1) * 16] = (1.0 / np.minimum(np.arange(1, 17), w)).astype(np.float32)[None, :]
    f = lambda a: np.ascontiguousarray(np.asarray(a, np.float32))

    def panels(w):
        w = np.asarray(w, np.float32)
        L, K_, M_ = w.shape
        k, m = K_ // 128, M_ // 128
        return np.ascontiguousarray(w.reshape(L, k, 128, m, 128).transpose(0, 3, 2, 1, 4)).reshape(L * m * 128, k * 128)

    def blocks(w):
        w = np.asarray(w, np.float32)
        J = w.shape[0]
        return np.ascontiguousarray(w.reshape(J, 16, 2, 128, 256).transpose(0, 1, 3, 2, 4)).reshape(J * 16 * 128, 512)
    return {
        "xT": f(np.asarray(x[b]).T),
        "pT": f(np.transpose(np.asarray(p[:, b]), (0, 2, 1)).reshape(DEPTH * PLE, S)),
        "w_in": panels(w_in),
        "w_out": panels(w_out),
        "w_gate": panels(w_ple_gate),
        "w_ple": panels(w_ple),
        "pool_w": panels(np.asarray(pool_w).reshape(8, 1024, 1024)),
        "lru_wa": blocks(lru_wa),
        "lru_wx": blocks(lru_wx),
        "pv": pv.reshape(DEPTH * 128, NPV),
        "cst": cst,
    }


N_LAUNCH_CORES = 2


def kernel(**inputs):
    nc = build_program()
    maps = [make_inputs(b, **inputs) for b in range(2)]
    in_maps = [maps[c % 2] for c in range(N_LAUNCH_CORES)]
    res = run_bass_kernel_spmd(nc, in_maps, core_ids=list(range(N_LAUNCH_CORES)))
    out = np.stack([np.ascontiguousarray(res.results[b]["yT"].T) for b in range(2)], axis=0)
    return out.astype(np.float32)
```
